# Optimizing a Trainium2 kernel written in Bass

```python
import math
import jax
import jax.numpy as jnp
from jax import lax
import numpy as np

D_MODEL = 1024
BATCH = 8
SEQ = 8192
DEPTH = 2

N_A = DEPTH // 2
N_B = DEPTH - N_A
PLE_DIM = 256
RWKV_HEAD = 64
RWKV_HEADS = D_MODEL // RWKV_HEAD
DECAY_LORA = 64
ICLR_LORA = 64
GN_EPS = 64e-5
DIFF_HEAD = 64
DIFF_HEADS = D_MODEL // (2 * DIFF_HEAD)
ROPE_DIMS = DIFF_HEAD // 4
ROPE_THETA = 500000.0
Q_BLOCK = 128
NORM_EPS = 1e-6
SUBLN_EPS = 1e-5

kernel_name = 'yoco_rwkv7_diffattn_sandwich_ple'


def rms_norm(x, g, eps=NORM_EPS):
    xf = x.astype(jnp.float32)
    y = xf * lax.rsqrt(jnp.mean(xf * xf, axis=-1, keepdims=True) + eps)
    return (y * g.astype(jnp.float32)).astype(x.dtype)


def rwkv7_time_mix(xn, mu, w_in, w0, w1, w2, a0, a1, a2, k_k, k_a, r_k, lnx_g, lnx_b, w_out):
    B, S, D = xn.shape
    H, N = RWKV_HEADS, RWKV_HEAD
    f32 = jnp.float32
    x_prev = jnp.pad(xn, ((0, 0), (1, 0), (0, 0)))[:, :-1]
    xmix = xn[None] + (x_prev - xn)[None] * mu[:, None, None, :]
    rkvg = jnp.einsum('cbsd,cde->cbse', xmix[:4], w_in)
    r, k, v, g = rkvg[0], rkvg[1], rkvg[2], rkvg[3]
    xw, xa = xmix[4], xmix[5]
    w_pre = (w0 + jnp.tanh(xw @ w1) @ w2).astype(f32)
    decay = jnp.exp(-jnp.exp(-jax.nn.softplus(-w_pre) - 0.5))
    a = jax.nn.sigmoid((a0 + (xa @ a1) @ a2).astype(f32))
    heads = lambda t: t.astype(f32).reshape(B, S, H, N)
    r, k, v, decay, a = heads(r), heads(k), heads(v), heads(decay), heads(a)
    kk = k * k_k.astype(f32).reshape(H, N)
    kk = kk / jnp.maximum(jnp.linalg.norm(kk, axis=-1, keepdims=True), 1e-12)
    k = k * (1.0 + (a - 1.0) * k_a.astype(f32).reshape(H, N))

    def step(state, inp):
        r_t, w_t, k_t, v_t, kk_t, a_t = inp
        s_kk = jnp.einsum('bhij,bhj->bhi', state, kk_t)
        state = (state * w_t[:, :, None, :]
                 - s_kk[..., None] * (kk_t * a_t)[:, :, None, :]
                 + v_t[..., None] * k_t[:, :, None, :])
        return state, jnp.einsum('bhij,bhj->bhi', state, r_t)

    seq_first = lambda t: jnp.swapaxes(t, 0, 1)
    state0 = jnp.zeros((B, H, N, N), f32)
    _, y = lax.scan(step, state0, (seq_first(r), seq_first(decay), seq_first(k),
                                   seq_first(v), seq_first(kk), seq_first(a)))
    y = jnp.swapaxes(y, 0, 1)
    mean = jnp.mean(y, axis=-1, keepdims=True)
    var = jnp.mean(jnp.square(y - mean), axis=-1, keepdims=True)
    y = ((y - mean) * lax.rsqrt(var + GN_EPS) * lnx_g.astype(f32).reshape(H, N)
         + lnx_b.astype(f32).reshape(H, N))
    y = y + jnp.sum(r * k * r_k.astype(f32), axis=-1, keepdims=True) * v
    y = y.reshape(B, S, D).astype(xn.dtype) * jax.nn.silu(g)
    return y @ w_out


def rope_tables(S):
    inv = ROPE_THETA ** (-jnp.arange(0, ROPE_DIMS, 2, dtype=jnp.float32) / ROPE_DIMS)
    ang = jnp.arange(S, dtype=jnp.float32)[:, None] * inv[None, :]
    return jnp.cos(ang), jnp.sin(ang)


def partial_rope(x, cos, sin):
    half = ROPE_DIMS // 2
    c = cos[None, :, None, None, :].astype(x.dtype)
    s = sin[None, :, None, None, :].astype(x.dtype)
    x1, x2 = x[..., :half], x[..., half:ROPE_DIMS]
    return jnp.concatenate([x1 * c - x2 * s, x2 * c + x1 * s, x[..., ROPE_DIMS:]], axis=-1)


def shared_kv(h, kv_norm, kv_w, cos, sin):
    B, S, D = h.shape
    kv = rms_norm(h, kv_norm) @ kv_w
    k = partial_rope(kv[..., :D].reshape(B, S, DIFF_HEADS, 2, DIFF_HEAD), cos, sin)
    v = kv[..., D:].reshape(B, S, DIFF_HEADS, 2 * DIFF_HEAD)
    return k.transpose(0, 2, 3, 1, 4), v.transpose(0, 2, 1, 3)


def diff_attention_mix(xn, k_sh, v_sh, w_in, lam_qk, subln_g, w_out, lam_init, cos, sin):
    B, S, D = xn.shape
    H, Dh = DIFF_HEADS, DIFF_HEAD
    f32 = jnp.float32
    proj = xn @ w_in
    q = partial_rope(proj[..., :D].reshape(B, S, H, 2, Dh), cos, sin) * (Dh ** -0.5)
    gate = proj[..., D:]
    lq = lam_qk.astype(f32)
    lam = jnp.exp(jnp.sum(lq[0] * lq[1])) - jnp.exp(jnp.sum(lq[2] * lq[3])) + lam_init
    nblk = S // Q_BLOCK
    qb = q.reshape(B, nblk, Q_BLOCK, H, 2, Dh).transpose(1, 0, 3, 4, 2, 5)
    kf = k_sh.astype(f32)
    vf = v_sh.astype(f32)
    k_pos = jnp.arange(S)

    def block(args):
        q_blk, blk = args
        s = jnp.einsum('bhcqd,bhckd->bhcqk', q_blk.astype(f32), kf)
        q_pos = blk * Q_BLOCK + jnp.arange(Q_BLOCK)
        s = jnp.where(k_pos[None, :] <= q_pos[:, None], s, -1e30)
        prob = jax.nn.softmax(s, axis=-1)
        attn = prob[:, :, 0] - lam * prob[:, :, 1]
        return jnp.einsum('bhqk,bhkv->bhqv', attn, vf)

    o = lax.map(block, (qb, jnp.arange(nblk)))
    o = o.transpose(1, 0, 3, 2, 4).reshape(B, S, H, 2 * Dh)
    o = rms_norm(o, subln_g, SUBLN_EPS) * (1.0 - lam_init)
    o = o.reshape(B, S, D).astype(xn.dtype) * jax.nn.silu(gate)
    return o @ w_out


def setup_inputs(seed: int = 0) -> dict:
    key = jax.random.key(seed)
    ks = iter(jax.random.split(key, 32))
    D, f32 = D_MODEL, jnp.float32
    nrm = lambda shape, scale: jax.random.normal(next(ks), shape, f32) * scale
    gain = lambda shape: 1.0 + nrm(shape, 0.02)
    return {
        'x': nrm((BATCH, SEQ, D), 1.0),
        'p': nrm((DEPTH, BATCH, SEQ, PLE_DIM), 1.0),
        'norm_pre': gain((DEPTH, D)),
        'norm_post': gain((DEPTH, D)),
        'a_mu': jax.random.uniform(next(ks), (N_A, 6, D), f32),
        'a_w_in': nrm((N_A, 4, D, D), D ** -0.5),
        'a_w0': jax.random.uniform(next(ks), (N_A, D), f32, -6.0, 1.0),
        'a_w1': nrm((N_A, D, DECAY_LORA), D ** -0.5),
        'a_w2': nrm((N_A, DECAY_LORA, D), 0.1),
        'a_a0': nrm((N_A, D), 0.5),
        'a_a1': nrm((N_A, D, ICLR_LORA), D ** -0.5),
        'a_a2': nrm((N_A, ICLR_LORA, D), 0.5 * ICLR_LORA ** -0.5),
        'a_k_k': 1.0 + nrm((N_A, D), 0.1),
        'a_k_a': 1.0 + nrm((N_A, D), 0.1),
        'a_r_k': nrm((N_A, RWKV_HEADS, RWKV_HEAD), 0.1),
        'a_lnx_g': gain((N_A, D)),
        'a_lnx_b': nrm((N_A, D), 0.02),
        'a_w_out': nrm((N_A, D, D), D ** -0.5),
        'kv_norm': gain((D,)),
        'kv_w': nrm((D, 2 * D), D ** -0.5),
        'b_w_in': nrm((N_B, D, 2 * D), D ** -0.5),
        'b_lambda': nrm((N_B, 4, DIFF_HEAD), 0.1),
        'b_subln': gain((N_B, 2 * DIFF_HEAD)),
        'b_w_out': nrm((N_B, D, D), D ** -0.5),
        'ple_w': nrm((DEPTH, PLE_DIM, D), PLE_DIM ** -0.5),
        'ple_gate': nrm((DEPTH, D, D), D ** -0.5),
        'ple_norm': gain((DEPTH, D)),
    }


def reference(x, p, norm_pre, norm_post, a_mu, a_w_in, a_w0, a_w1, a_w2, a_a0, a_a1, a_a2,
              a_k_k, a_k_a, a_r_k, a_lnx_g, a_lnx_b, a_w_out, kv_norm, kv_w, b_w_in, b_lambda,
              b_subln, b_w_out, ple_w, ple_gate, ple_norm):
    S = x.shape[1]
    cos, sin = rope_tables(S)
    h = x
    k_sh = None
    v_sh = None
    for i in range(DEPTH):
        xn = rms_norm(h, norm_pre[i])
        if i < N_A:
            y = rwkv7_time_mix(xn, a_mu[i], a_w_in[i], a_w0[i], a_w1[i], a_w2[i], a_a0[i],
                               a_a1[i], a_a2[i], a_k_k[i], a_k_a[i], a_r_k[i], a_lnx_g[i],
                               a_lnx_b[i], a_w_out[i])
        else:
            if i == N_A:
                k_sh, v_sh = shared_kv(h, kv_norm, kv_w, cos, sin)
            j = i - N_A
            lam_init = 0.8 - 0.6 * math.exp(-0.3 * i)
            y = diff_attention_mix(xn, k_sh, v_sh, b_w_in[j], b_lambda[j], b_subln[j],
                                   b_w_out[j], lam_init, cos, sin)
        h = h + rms_norm(y, norm_post[i])
        e = p[i] @ ple_w[i]
        g = jax.nn.sigmoid(h @ ple_gate[i])
        h = h + rms_norm(g * e, ple_norm[i])
    return h
```

```python
import numpy as np
import concourse.bass as bass
import concourse.mybir as mybir

F32 = mybir.dt.float32
BF16 = mybir.dt.bfloat16
AF = mybir.ActivationFunctionType
ALU = mybir.AluOpType
AX = mybir.AxisListType

ENGS = ("tensor", "vector", "scalar", "gpsimd", "sync")
EPOCH = 12000


class Buf:
    __slots__ = ("name", "w", "r", "prev_r", "excl")

    def __init__(self, name):
        self.name = name
        self.excl = False
        self.w = []
        self.r = []
        self.prev_r = []


class Op:
    __slots__ = ("eng", "fn", "deps", "dma_key", "signal", "tok", "idx")

    def __init__(self, eng, fn, dma_key=None):
        self.eng = eng
        self.fn = fn
        self.deps = []
        self.dma_key = dma_key
        self.signal = dma_key is not None
        self.tok = None


class PEProxy:
    def __init__(self, eng, fence):
        self.eng = eng
        self.fence = fence
        self.prev = None
        self.nfence = 0

    def _chk(self, ap):
        key = (ap.base_partition(), ap.shape[0])
        partial = key[1] < 128
        if partial and self.prev is not None and self.prev != key and self.fence is not None:
            self.fence(self.eng)
            self.nfence += 1
        self.prev = key if partial else None

    def matmul(self, out, lhsT, rhs, **kw):
        self._chk(lhsT)
        return self.eng.matmul(out, lhsT=lhsT, rhs=rhs, **kw)

    def transpose(self, out, in_, identity, **kw):
        self._chk(in_)
        return self.eng.transpose(out=out, in_=in_, identity=identity, **kw)

    def wait_ge(self, *a, **k):
        return self.eng.wait_ge(*a, **k)


class Ctx:
    pe_fence = None

    def __init__(self, nc, same_engine_sync=True):
        self.nc = nc
        self.ops = {e: [] for e in ENGS}
        self.same_engine_sync = same_engine_sync
        self.nbuf = 0

    def buf(self, name=None):
        self.nbuf += 1
        return Buf(name or f"b{self.nbuf}")

    def op(self, eng, fn, reads=(), writes=(), appends=(), dma_key=None):
        o = Op(eng, fn, dma_key)
        deps = o.deps
        for b in reads:
            deps.extend(b.w)
            if b.excl:
                deps.extend(x for x in b.r if x.eng != eng)
        for b in writes:
            deps.extend(b.w)
            deps.extend(b.r)
        for b in appends:
            deps.extend(b.prev_r)
            deps.extend(b.w[:1])
        for b in reads:
            b.r.append(o)
        for b in writes:
            b.prev_r = b.r
            b.w = [o]
            b.r = []
        for b in appends:
            b.w.append(o)
        self.ops[eng].append(o)
        return o

    def dma(self, eng, out, in_, reads=(), writes=(), appends=(), key=None, slow=False):
        if slow:
            fn = lambda e: e.dma_start(out=out, in_=in_, allow_slow_non_contiguous=True)
        else:
            fn = lambda e: e.dma_start(out=out, in_=in_)
        return self.op(eng, fn, reads, writes, appends, dma_key=key or "dma_" + eng)

    def barrier(self):
        lasts = []
        for e in ENGS:
            ops = self.ops[e]
            for o in reversed(ops):
                if o.dma_key is None and o.fn is not None:
                    lasts.append(o)
                    break
            seen = set()
            for o in reversed(ops):
                if o.dma_key is not None and o.dma_key not in seen:
                    seen.add(o.dma_key)
                    lasts.append(o)
        for e in ENGS:
            o = Op(e, None)
            o.deps = list(lasts)
            self.ops[e].append(o)

    def emit(self, final_wait_ops=()):
        nc = self.nc
        for e in ENGS:
            for o in self.ops[e]:
                for d in o.deps:
                    if d.eng == e and d.dma_key is None:
                        if e == "tensor" or not self.same_engine_sync:
                            continue
                    d.signal = True
        for o in final_wait_ops:
            o.signal = True
        sems = {}
        import contextlib
        stack = contextlib.ExitStack()

        def getsem(key):
            if key not in sems:
                sems[key] = stack.enter_context(nc.semaphore(key))
            return sems[key]

        dma_cnt = {}
        for e in ENGS:
            cnt = 0
            for o in self.ops[e]:
                if o.fn is None:
                    continue
                if o.dma_key is not None:
                    dma_cnt[o.dma_key] = dma_cnt.get(o.dma_key, 0) + 16
                    c = dma_cnt[o.dma_key]
                    ep, v = divmod(c - 16, EPOCH * 16)
                    o.tok = (f"{o.dma_key}_{ep}", v + 16)
                elif o.signal:
                    ep, v = divmod(cnt, EPOCH)
                    o.tok = (f"s_{e}_{ep}", v + 1)
                    cnt += 1
        for e in ENGS:
            for o in self.ops[e]:
                if o.tok is not None:
                    getsem(o.tok[0])
        self.nsem = len(sems)
        with stack, nc.Block() as block:
            def run(e, engobj, extra_final=False):
                seen = {}
                nwait = 0
                for o in self.ops[e]:
                    need = {}
                    for d in o.deps:
                        if d.tok is None:
                            continue
                        if d.eng == e and d.dma_key is None and (
                                e == "tensor" or not self.same_engine_sync):
                            continue
                        k, v = d.tok
                        if seen.get(k, 0) < v and need.get(k, 0) < v:
                            need[k] = v
                    for k, v in need.items():
                        engobj.wait_ge(sems[k], v)
                        seen[k] = v
                        nwait += 1
                    if o.fn is None:
                        continue
                    ins = o.fn(engobj)
                    if o.tok is not None:
                        ins.then_inc(sems[o.tok[0]], 16 if o.dma_key is not None else 1)
                if extra_final:
                    for o in final_wait_ops:
                        k, v = o.tok
                        if seen.get(k, 0) < v:
                            engobj.wait_ge(sems[k], v)
                            seen[k] = v

            @block.tensor
            def _(t):
                px = PEProxy(t, self.pe_fence)
                run("tensor", px)
                self.nfence = px.nfence

            @block.vector
            def _(v):
                run("vector", v)

            @block.scalar
            def _(s):
                run("scalar", s)

            @block.gpsimd
            def _(g):
                run("gpsimd", g)

            @block.sync
            def _(s):
                run("sync", s, extra_final=True)


import math
import numpy as np
import ml_dtypes

D = 1024
PLE = 256
NH = 16
HD = 64
C = 64
DECAY_C = -math.exp(-0.5)
GN_EPS = 64e-5
NORM_EPS = 1e-6
SUBLN_EPS = 1e-5


class Mem:
    def __init__(self, nc, ctx, limit=229376):
        self.nc = nc
        self.ctx = ctx
        self.off = 16640
        self.limit = limit
        self.n = 0
        self.peak = 0

    def alloc(self, name, shape, dtype):
        sz = 4 if dtype == F32 else 2
        nbytes = int(np.prod(shape[1:])) * sz
        nbytes = (nbytes + 63) // 64 * 64
        self.n += 1
        t = self.nc.alloc_sbuf_tensor_at(f"{name}_{self.n}", list(shape), dtype, offset=self.off)
        self.off += nbytes
        self.peak = max(self.peak, self.off)
        assert self.off <= self.limit, f"SBUF overflow at {name}: {self.off}"
        return t, self.ctx.buf(name)

    def mark(self):
        return self.off

    def reset(self, off):
        self.off = off


class T:
    def __init__(self, h, buf):
        self.h = h
        self.buf = buf

    def __getitem__(self, k):
        return self.h[k]


def make_consts():
    c = {}
    c["ident"] = np.eye(128, dtype=np.float32)
    s = np.arange(128)[:, None]
    t = np.arange(128)[None, :]
    same = (s // C) == (t // C)
    incl = (same & (s <= t)).astype(np.float32)
    strict = (same & (s < t)).astype(np.float32)
    suffix = (same & (s > t)).astype(np.float32)
    c["tri"] = np.stack([incl * DECAY_C, strict * DECAY_C, suffix * DECAY_C], 1).reshape(128, 384).astype(np.float32)
    sel = np.zeros((128, 2), np.float32)
    sel[:64, 0] = DECAY_C
    sel[64:, 1] = DECAY_C
    c["sel"] = sel
    m_si = np.concatenate([strict, incl], 1)
    m_tt = np.concatenate([strict.T, strict.T], 1)
    c["mask_si"] = np.concatenate([m_si, m_si], 1).astype(np.float32)
    c["mask_tt"] = np.concatenate([m_tt, m_tt], 1).astype(np.float32)
    c["mask_kq"] = (t >= s).astype(np.float32)
    return c


def rope_tables(S):
    inv = 500000.0 ** (-np.arange(0, 16, 2, dtype=np.float32) / 16)
    ang = np.arange(S, dtype=np.float32)[:, None] * inv[None, :]
    return np.cos(ang).astype(np.float32), np.sin(ang).astype(np.float32)


class KB:
    def __init__(self, nc, S):
        self.nc = nc
        self.S = S
        self.c = Ctx(nc)
        self.mem = Mem(nc, self.c)
        self.ps = []
        for i in range(8):
            h = nc.alloc_psum_tensor(f"psb{i}", [128, 512], F32)
            self.ps.append(T(h, self.c.buf(f"ps{i}")))
            self.ps[-1].buf.excl = True
        self.rr = 0
        self.rot = [0, 1, 2, 3, 4, 5, 6]
        self.cv = 0

    def bank(self):
        b = self.ps[self.rot[self.rr % len(self.rot)]]
        self.rr += 1
        return b

    def op(self, eng, fn, r=(), w=(), a=()):
        return self.c.op(eng, fn, [t.buf for t in r], [t.buf for t in w], [t.buf for t in a])

    def dma(self, eng, out, in_, r=(), w=(), a=(), key=None, slow=False):
        return self.c.dma(eng, out, in_, [t.buf for t in r], [t.buf for t in w],
                          [t.buf for t in a], key=key, slow=slow)

    def alloc(self, name, shape, dtype):
        h, b = self.mem.alloc(name, shape, dtype)
        return T(h, b)

    def cvt_eng(self):
        self.cv += 1
        return ("gpsimd", "vector")[self.cv % 2]

    def load_consts(self, dram):
        al = self.alloc
        self.ident_f = al("ident_f", [128, 128], F32)
        self.ident = al("ident", [128, 128], BF16)
        self.tri = al("tri", [128, 384], F32)
        self.sel = al("sel", [128, 2], F32)
        self.mask_si = al("mask_si", [128, 512], BF16)
        self.mask_tt = al("mask_tt", [128, 512], BF16)
        self.mask_kq = al("mask_kq", [128, 128], BF16)
        self.ones_f = al("ones_f", [1, 128], F32)
        stg = al("cstg", [128, 512], F32)
        self.dma("sync", self.ident_f[:, :], dram["c_ident"], w=[self.ident_f], key="c0")
        self.op("vector", lambda e: e.tensor_copy(out=self.ident[:, :], in_=self.ident_f[:, :]),
                r=[self.ident_f], w=[self.ident])
        self.dma("sync", self.tri[:, :], dram["c_tri"], w=[self.tri], key="c1")
        self.dma("sync", self.sel[:, :], dram["c_sel"], w=[self.sel], key="c2")
        for nm, dst, n in (("c_mask_si", self.mask_si, 512), ("c_mask_tt", self.mask_tt, 512),
                           ("c_mask_kq", self.mask_kq, 128)):
            self.dma("sync", stg[:, 0:n], dram[nm], w=[stg], key="c3")
            self.op("vector", lambda e, dst=dst, n=n: e.tensor_copy(out=dst[:, :], in_=stg[:, 0:n]),
                    r=[stg], w=[dst])
        self.op("vector", lambda e: e.memset(self.ones_f[:, :], 1.0), w=[self.ones_f])
        fz = al("fz", [128, 32], BF16)
        self.op("vector", lambda e: e.memset(fz[:, :], 0.0), w=[fz])
        scr = self.ps[7]

        def fence(e):
            return e.matmul(scr[0:32, 0:8], lhsT=fz[:, 0:32], rhs=fz[:, 0:8], start=True, stop=True)
        self.op("tensor", fence, r=[fz])
        self.c.pe_fence = fence

    def load_w_bf16(self, dst, src, ncols, stg, col0=0, nck=8):
        srcv = src.rearrange("(c p) n -> p c n", p=128)
        for ck in range(nck):
            for n0 in range(0, ncols, 1024):
                n1 = min(ncols, n0 + 1024)
                st = stg[self.cv % len(stg)]
                self.dma("sync", st[:, 0:n1 - n0], srcv[:, ck, n0:n1], w=[st], key=f"wst{self.cv % len(stg)}")
                eng = self.cvt_eng()
                self.op(eng, lambda e, st=st, ck=ck, n0=n0, n1=n1: e.tensor_copy(
                    out=dst[:, ck, col0 + n0:col0 + n1], in_=st[:, 0:n1 - n0]), r=[st], a=[dst])

    def load_bcast(self, dst, src_vec, stg=None, key="bc"):
        n = src_vec.shape[0]
        if stg is None:
            self.dma("sync", dst[:, :], src_vec.partition_broadcast(128), w=[dst], key=key)
        else:
            self.dma("sync", stg[:, 0:n], src_vec.partition_broadcast(128), w=[stg], key=key)
            self.op("vector", lambda e: e.tensor_copy(out=dst[:, :], in_=stg[:, 0:n]), r=[stg], w=[dst])

    def load_fm_vecs(self, dst, vecs, stg):
        nv = len(vecs)
        rows = stg
        for i, v in enumerate(vecs):
            self.dma("sync", rows[8 * i:8 * i + 8, 0:128], v.rearrange("(c p) -> c p", p=128),
                     w=[rows] if i == 0 else [], a=[rows] if i else [], key="fmv")
        bk = self.bank()
        n = nv * 8
        self.op("tensor", lambda e: e.transpose(out=bk[:, 0:n], in_=rows[0:n, 0:128], identity=self.ident_f[0:n, 0:n]),
                r=[rows, self.ident_f], w=[bk])
        self.op("vector", lambda e: e.tensor_copy(out=dst[:, :, :].rearrange("p v c -> p (v c)"), in_=bk[:, 0:n]),
                r=[bk], w=[dst])

    def transpose8(self, dst_fn, src, nck=8, evac=None):
        bk = self.bank()
        pv = bk[:, :].bitcast(BF16).rearrange("p (c t) -> p c t", t=128)
        for ck in range(nck):
            self.op("tensor", lambda e, ck=ck: e.transpose(out=pv[:, ck, :], in_=src[:, ck * 128:(ck + 1) * 128],
                                                           identity=self.ident[:, :]),
                    r=[src, self.ident], w=[bk] if ck == 0 else [], a=[bk] if ck else [])
        dst_fn(bk, pv)


def phase_a1(kb, dram, yg_out, stop=None, dbg=None):
    nc, S = kb.nc, kb.S
    al = kb.alloc
    op = kb.op
    NT = S // 128
    m0 = kb.mem.mark()
    Win = al("Win", [128, 8, 4096], BF16)
    W1A1 = al("W1A1", [128, 8, 128], BF16)
    W2A2 = al("W2A2", [128, 1024], BF16)
    w0hl = al("w0hl", [33, 2048], BF16)
    ones_b = al("ones_b", [33, 128], BF16)
    mu_fm = al("mu_fm", [128, 7, 8], F32)
    bc_kk = al("bc_kk", [128, 1024], BF16)
    bc_ka = al("bc_ka", [128, 1024], BF16)
    bc_rk = al("bc_rk", [128, 1024], BF16)
    bc_lg = al("bc_lg", [128, 1024], BF16)
    bc_lb = al("bc_lb", [128, 1024], BF16)
    m1 = kb.mem.mark()
    stg = [al(f"wstg{i}", [128, 1024], F32) for i in range(3)]
    for ci in range(4):
        kb.load_w_bf16(Win, dram["a_w_in"][ci], 1024, stg, col0=ci * 1024)
    kb.load_w_bf16(W1A1, dram["a_w1"], 64, stg, col0=0)
    kb.load_w_bf16(W1A1, dram["a_a1"], 64, stg, col0=64)
    st = stg[0]
    kb.dma("sync", st[0:64, 0:1024], dram["a_w2"], w=[st], key="wst0")
    kb.dma("sync", st[64:128, 0:1024], dram["a_a2"], a=[st], key="wst0")
    op("vector", lambda e: e.tensor_copy(out=W2A2[:, :], in_=st[:, :]), r=[st], w=[W2A2])
    op("vector", lambda e: e.memset(w0hl[:, :], 0.0), w=[w0hl])
    op("vector", lambda e: e.memset(ones_b[:, :], 1.0), w=[ones_b])
    sA, sB = stg[1], stg[2]
    for pr in (0, 32):
        kb.dma("sync", sA[pr:pr + 1, 0:1024], dram["a_w0"].unsqueeze(0), w=[sA] if pr == 0 else [], a=[sA] if pr else [], key="sm0")
        kb.dma("sync", sB[pr:pr + 1, 0:1024], dram["a_a0"].unsqueeze(0), w=[sB] if pr == 0 else [], a=[sB] if pr else [], key="sm3")
    for src_, c0 in ((sA, 0), (sB, 1024)):
        op("vector", lambda e, src_=src_, c0=c0: e.tensor_copy(out=w0hl[0:1, c0:c0 + 1024], in_=src_[0:1, 0:1024]),
           r=[src_], a=[w0hl])
        op("vector", lambda e, src_=src_, c0=c0: e.tensor_copy(out=w0hl[32:33, c0:c0 + 1024], in_=src_[32:33, 0:1024]),
           r=[src_], a=[w0hl])
        op("vector", lambda e, src_=src_, c0=c0: e.tensor_tensor(out=src_[32:33, 0:1024], in0=src_[32:33, 0:1024],
                                                                in1=w0hl[32:33, c0:c0 + 1024], op=ALU.subtract),
           r=[w0hl, src_], w=[src_])
        op("vector", lambda e, src_=src_, c0=c0: e.tensor_copy(out=w0hl[32:33, c0:c0 + 1024], in_=src_[32:33, 0:1024]),
           r=[src_], w=[w0hl])
    kb.load_fm_vecs(mu_fm, [dram["a_mu"][ci] for ci in range(6)] + [dram["norm_pre"][0]], stg[2])
    for dst, nm in ((bc_kk, "a_k_k"), (bc_ka, "a_k_a"), (bc_rk, "a_r_k"), (bc_lg, "a_lnx_g"), (bc_lb, "a_lnx_b")):
        kb.load_bcast(dst, dram[nm], stg=stg[1])
    kb.c.barrier()
    kb.mem.reset(m1)
    class Stop(Exception):
        pass
    def dump(k, t, f32=False):
        if stop == k:
            v = t[:, :] if len(t.h.shape) == 2 else t[:, :, :].rearrange("p a b -> p (a b)")
            n = v.shape[1]
            kb.last = kb.dma("sync", (dbg if f32 else yg_out)[0:128, 0:n], v, r=[t], key="dbg")
            raise Stop()
    dump(0, W2A2)
    xt = [al("xt0", [128, 1024], F32)]
    xs = al("xs", [128, 1024], BF16)
    xnT = [al(f"xnT{i}", [128, 8, 129], BF16) for i in range(2)]
    dxT = al("dxT", [128, 8, 128], BF16)
    tmpm = al("tmpm", [128, 8, 128], BF16)
    xm = [al(f"xm{i}", [128, 8, 128], BF16) for i in range(2)]
    F = [al(f"F{i}", [128, 1024], F32) for i in range(5)]
    Vb = al("Vb", [128, 1024], BF16)
    sgb = al("sgb", [128, 1024], BF16)
    E = [al(f"E{i}", [128, 1024], BF16) for i in range(4)]
    O = [al(f"O{i}", [128, 1024], BF16) for i in range(6)]
    hT = al("hT", [128, 128], BF16)
    ARTz = [al(f"ARTz{i}", [128, 8, 2, 128], BF16) for i in range(2)]
    BKT = al("BKT", [128, 8, 2, 128], BF16)
    gC = [al(f"gC{i}", [128, 8, 2], F32) for i in range(2)]
    sm = [al(f"sm{i}", [128, 16], F32) for i in range(6)]
    ss = al("ss", [128, 2], F32)
    MIs = [al(f"MI{s_}", [128, 4, 2, 128], BF16) for s_ in range(2)]
    MKs = [al(f"MK{s_}", [128, 4, 2, 128], BF16) for s_ in range(2)]
    MTs = [al(f"MT{s_}", [128, 4, 2, 128], BF16) for s_ in range(2)]
    Nks = [[al(f"Nk{s_}_{i}", [128, 4, 128], BF16) for i in range(2)] for s_ in range(2)]
    NkTs = [[al(f"NkT{s_}_{i}", [128, 4, 128], BF16) for i in range(2)] for s_ in range(2)]
    Tks = [[al(f"Tk{s_}_{i}", [128, 4, 128], BF16) for i in range(2)] for s_ in range(2)]
    Gs = [al(f"G{s_}", [128, 4, 128], BF16) for s_ in range(2)]
    Stmps = [al(f"Stmp{s_}", [128, 2, 64], F32) for s_ in range(2)]
    WTh = [al(f"WT{g_}", [128, 256], BF16) for g_ in range(4)]
    Abh = [al(f"Ab{g_}", [128, 256], BF16) for g_ in range(4)]
    RbTh = [al(f"RbT{g_}", [128, 2, 128], BF16) for g_ in range(4)]
    Pph = [al(f"Pp{g_}", [128, 2, 2, 64], F32) for g_ in range(4)]
    Sth = [al(f"St{g_}", [128, 2, 64], F32) for g_ in range(4)]
    Sbh = [al(f"Sb{g_}", [128, 2, 64], BF16) for g_ in range(4)]
    ygt = al("ygt", [128, 1024], BF16)
    print("A1 sbuf peak", kb.mem.peak)
    ident = kb.ident

    for z_ in ARTz:
        op("gpsimd", lambda e, z_=z_: e.memset(z_[:, :, :, :], 0.0), w=[z_])
    for g_ in range(4):
        op("vector", lambda e, g_=g_: e.memset(Sth[g_][:, :, :], 0.0), w=[Sth[g_]])
        op("vector", lambda e, g_=g_: e.memset(Sbh[g_][:, :, :], 0.0), w=[Sbh[g_]])
    kb.rot = [0, 1, 2, 3]
    kb.rr = 0
    op("gpsimd", lambda e: e.memset(xnT[1][:, :, :], 0.0), w=[xnT[1]])
    op("gpsimd", lambda e: e.memset(xnT[0][:, :, :], 0.0), w=[xnT[0]])

    xv = dram["x"]
    kb.dma("sync", xt[0][:, :], xv[0:128, :], w=[xt[0]], key="ldx0")
    def mix(ci, dst, xc):
        op("vector", lambda e: e.tensor_tensor(out=tmpm[:, :, :], in0=dxT[:, :, :],
                                               in1=mu_fm[:, ci, :].unsqueeze(2).broadcast_to([128, 8, 128]),
                                               op=ALU.mult), r=[dxT, mu_fm], w=[tmpm])
        op("vector", lambda e: e.tensor_tensor(out=dst[:, :, :], in0=tmpm[:, :, :], in1=xc[:, :, 1:129],
                                               op=ALU.add), r=[tmpm, xc], w=[dst])

    def part1_g(it):
        cur = xt[0]
        xc, xp = xnT[it % 2], xnT[(it + 1) % 2]
        gCc = gC[it % 2]
        op("gpsimd", lambda e: e.memset(ss[:, :], 0.0), w=[ss])
        op("scalar", lambda e: e.activation(out=xs[:, :], in_=cur[:, :], func=AF.Square, scale=1.0 / 32,
                                            accum_out=ss[:, 0:1]), r=[cur], w=[xs, ss])
        op("scalar", lambda e: e.activation(out=ss[:, 1:2], in_=ss[:, 0:1], func=AF.Sqrt, bias=NORM_EPS, scale=1.0),
           r=[ss], w=[ss])
        op("vector", lambda e: e.reciprocal(out=ss[:, 1:2], in_=ss[:, 1:2]), r=[ss], w=[ss])
        op("vector", lambda e: e.tensor_scalar(out=xs[:, :], in0=cur[:, :], scalar1=ss[:, 1:2], scalar2=None,
                                               op0=ALU.mult), r=[cur, ss], w=[xs])
        if it + 1 < NT:
            kb.dma("sync", cur[:, :], xv[(it + 1) * 128:(it + 2) * 128, :], w=[cur], key="ldx0")
        yield

        def ev_xn(bk, pv, xc=xc, xp=xp):
            op("vector", lambda e: e.tensor_tensor(out=xc[:, :, 1:129], in0=pv,
                                                   in1=mu_fm[:, 6, :].unsqueeze(2).broadcast_to([128, 8, 128]),
                                                   op=ALU.mult), r=[bk, mu_fm], w=[xc])
            op("gpsimd", lambda e: e.tensor_copy(out=xc[:, :, 0:1], in_=xp[:, :, 128:129]), r=[xp], a=[xc])
        kb.transpose8(ev_xn, xs)
        yield
        op("vector", lambda e: e.tensor_tensor(out=dxT[:, :, :], in0=xc[:, :, 0:128], in1=xc[:, :, 1:129],
                                               op=ALU.subtract), r=[xc], w=[dxT])
        yield
        bkh = kb.ps[4]
        for li, ci in enumerate((4, 5)):
            xmd = xm[li]
            mix(ci, xmd, xc)
            for ck in range(8):
                first = (ck == 0)
                op("tensor", lambda e, ck=ck, li=li, xmd=xmd, first=first: e.matmul(
                    bkh[64 * li:64 * li + 64, 0:128], lhsT=W1A1[:, ck, 64 * li:64 * li + 64], rhs=xmd[:, ck, :],
                    start=first, stop=(ck == 7)), r=[W1A1, xmd],
                   w=[bkh] if (first and li == 0) else [], a=[] if (first and li == 0) else [bkh])
            yield
        op("scalar", lambda e: e.activation(out=hT[0:64, :], in_=bkh[0:64, 0:128], func=AF.Tanh), r=[bkh], w=[hT])
        op("scalar", lambda e: e.copy(out=hT[64:128, :], in_=bkh[64:128, 0:128]), r=[bkh], a=[hT])
        yield
        for li, dst in ((0, F[2]), (1, F[4])):
            for half in range(2):
                bk = kb.bank()
                cs = slice(half * 512, half * 512 + 512)
                op("tensor", lambda e, bk=bk, li=li, cs=cs: e.matmul(
                    bk[:, :], lhsT=hT[64 * li:64 * li + 64, :], rhs=W2A2[64 * li:64 * li + 64, cs],
                    start=True, stop=False), r=[hT, W2A2], w=[bk])
                op("tensor", lambda e, bk=bk, li=li, half=half: e.matmul(
                    bk[:, :], lhsT=ones_b[0:33, :], rhs=w0hl[0:33, li * 1024 + half * 512: li * 1024 + half * 512 + 512],
                    start=False, stop=True), r=[ones_b, w0hl], a=[bk])
                op("scalar", lambda e, bk=bk, dst=dst, cs=cs: e.activation(out=dst[:, cs], in_=bk[:, :], func=AF.Sigmoid),
                   r=[bk], w=[dst] if half == 0 else [], a=[dst] if half else [])
                yield
        sgw = F[2]
        expcfg = ((0, E[0], 1.0), (0, E[1], -1.0), (1, E[2], 1.0), (2, E[3], 1.0))
        for half in range(2):
            cs = slice(half * 512, half * 512 + 512)
            for ti in range(3):
                bk = kb.bank()
                op("tensor", lambda e, bk=bk, ti=ti, cs=cs: e.matmul(
                    bk[:, :], lhsT=kb.tri[:, ti * 128:(ti + 1) * 128], rhs=sgw[:, cs], start=True, stop=True),
                   r=[kb.tri, sgw], w=[bk])
                for (tj, dst, sc) in expcfg:
                    if tj != ti:
                        continue
                    op("scalar", lambda e, bk=bk, dst=dst, sc=sc, cs=cs: e.activation(
                        out=dst[:, cs], in_=bk[:, :], func=AF.Exp, scale=sc), r=[bk],
                       w=[dst] if half == 0 else [], a=[dst] if half else [])
                yield
        bk = kb.bank()
        for ck in range(8):
            op("tensor", lambda e, ck=ck, bk=bk: e.matmul(bk[:, 2 * ck:2 * ck + 2], lhsT=sgw[:, ck * 128:(ck + 1) * 128],
                                                          rhs=kb.sel[:, :], start=True, stop=True),
               r=[sgw, kb.sel], w=[bk] if ck == 0 else [], a=[bk] if ck else [])
        op("scalar", lambda e, bk=bk: e.activation(out=gCc[:, :, :].rearrange("p c t -> p (c t)"), in_=bk[:, 0:16],
                                                   func=AF.Exp), r=[bk], w=[gCc])
        yield
    def tile(it):
        xc = xnT[it % 2]
        gCc = gC[it % 2]
        if it == 0:
            for _ in part1_g(0):
                pass
        sgw, asig = F[2], F[4]
        rT, kT = F[0], F[1]
        for pi, ci in enumerate((0, 1, 2, 3)):
            xmd = xm[pi % 2]
            mix(ci, xmd, xc)
            for half in range(2):
                bk = kb.bank()
                cs = slice(half * 512, half * 512 + 512)
                for ck in range(8):
                    op("tensor", lambda e, bk=bk, ck=ck, xmd=xmd, ci=ci, half=half: e.matmul(
                        bk[:, :], lhsT=xmd[:, ck, :], rhs=Win[:, ck, ci * 1024 + half * 512: ci * 1024 + half * 512 + 512],
                        start=(ck == 0), stop=(ck == 7)), r=[xmd, Win], w=[bk] if ck == 0 else [], a=[bk] if ck else [])
                if ci == 0:
                    op("scalar", lambda e, bk=bk, cs=cs: e.copy(out=rT[:, cs], in_=bk[:, :]), r=[bk],
                       w=[rT] if half == 0 else [], a=[rT] if half else [])
                elif ci == 1:
                    op("vector", lambda e, bk=bk, cs=cs: e.tensor_copy(out=kT[:, cs], in_=bk[:, :]), r=[bk],
                       w=[kT] if half == 0 else [], a=[kT] if half else [])
                elif ci == 2:
                    op("scalar", lambda e, bk=bk, cs=cs: e.copy(out=Vb[:, cs], in_=bk[:, :]), r=[bk],
                       w=[Vb] if half == 0 else [], a=[Vb] if half else [])
                else:
                    op("scalar", lambda e, bk=bk, cs=cs: e.activation(out=sgb[:, cs], in_=bk[:, :], func=AF.Silu),
                       r=[bk], w=[sgb] if half == 0 else [], a=[sgb] if half else [])
        dump(6, sgb)
        eL, enL, eLx, eD = E
        h3 = lambda t: t[:, :].rearrange("p (h j) -> p h j", j=64)
        bc3 = lambda t: t[:, :].unsqueeze(2).broadcast_to([128, 16, 64])
        kk = F[2]
        op("vector", lambda e: e.tensor_tensor(out=kk[:, :], in0=kT[:, :], in1=bc_kk[:, :], op=ALU.mult),
           r=[kT, bc_kk], w=[kk])
        op("vector", lambda e: e.tensor_tensor(out=F[3][:, :], in0=kk[:, :], in1=kk[:, :], op=ALU.mult),
           r=[kk], w=[F[3]])
        op("vector", lambda e: e.tensor_reduce(out=sm[0][:, :], in_=h3(F[3]), axis=AX.X, op=ALU.add),
           r=[F[3]], w=[sm[0]])
        op("scalar", lambda e: e.activation(out=sm[0][:, :], in_=sm[0][:, :], func=AF.Sqrt), r=[sm[0]], w=[sm[0]])
        op("vector", lambda e: e.tensor_scalar_max(out=sm[0][:, :], in0=sm[0][:, :], scalar1=1e-12), r=[sm[0]], w=[sm[0]])
        op("vector", lambda e: e.reciprocal(out=sm[0][:, :], in_=sm[0][:, :]), r=[sm[0]], w=[sm[0]])
        op("vector", lambda e: e.tensor_tensor(out=h3(kk), in0=h3(kk), in1=bc3(sm[0]), op=ALU.mult),
           r=[kk, sm[0]], w=[kk])
        bT = F[3]
        op("gpsimd", lambda e: e.tensor_tensor(out=bT[:, :], in0=kk[:, :], in1=asig[:, :], op=ALU.mult),
           r=[kk, asig], w=[bT])
        kmod = F[4]
        op("vector", lambda e: e.scalar_tensor_tensor(out=kmod[:, :], in0=asig[:, :], scalar=-1.0, in1=bc_ka[:, :],
                                                      op0=ALU.add, op1=ALU.mult), r=[asig, bc_ka], w=[kmod])
        op("vector", lambda e: e.scalar_tensor_tensor(out=kmod[:, :], in0=kmod[:, :], scalar=1.0, in1=kT[:, :],
                                                      op0=ALU.add, op1=ALU.mult), r=[kmod, kT], w=[kmod])
        rk = F[1]
        op("vector", lambda e: e.tensor_tensor(out=rk[:, :], in0=rT[:, :], in1=kmod[:, :], op=ALU.mult),
           r=[rT, kmod], w=[rk])
        op("vector", lambda e: e.tensor_tensor(out=rk[:, :], in0=rk[:, :], in1=bc_rk[:, :], op=ALU.mult),
           r=[rk, bc_rk], w=[rk])
        bsum = sm[1]
        op("vector", lambda e: e.tensor_reduce(out=bsum[:, :], in_=h3(rk), axis=AX.X, op=ALU.add), r=[rk], w=[bsum])
        Rt, Kt, Bt, At, Kg, Bg = O
        op("vector", lambda e: e.tensor_tensor(out=Rt[:, :], in0=rT[:, :], in1=eL[:, :], op=ALU.mult), r=[rT, eL], w=[Rt])
        op("gpsimd", lambda e: e.tensor_tensor(out=Kt[:, :], in0=kmod[:, :], in1=enL[:, :], op=ALU.mult), r=[kmod, enL], w=[Kt])
        op("vector", lambda e: e.tensor_tensor(out=Bt[:, :], in0=bT[:, :], in1=enL[:, :], op=ALU.mult), r=[bT, enL], w=[Bt])
        op("vector", lambda e: e.scalar_tensor_tensor(out=At[:, :], in0=kk[:, :], scalar=-1.0, in1=eLx[:, :],
                                                      op0=ALU.mult, op1=ALU.mult), r=[kk, eLx], w=[At])
        op("vector", lambda e: e.tensor_tensor(out=Kg[:, :], in0=kmod[:, :], in1=eD[:, :], op=ALU.mult), r=[kmod, eD], w=[Kg])
        op("gpsimd", lambda e: e.tensor_tensor(out=Bg[:, :], in0=bT[:, :], in1=eD[:, :], op=ALU.mult), r=[bT, eD], w=[Bg])
        dump(7, Bg)
        for src, dst, slot, eng in ((At, None, 0, "scalar"), (Rt, None, 1, "vector"), (Bt, BKT, 0, "scalar"), (Kt, BKT, 1, "vector")):
            def ev(bk, pv, dst=dst, slot=slot, eng=eng):
                if dst is None:
                    for par in range(2):
                        z_ = ARTz[par]
                        pr = slice(64 * par, 64 * par + 64)
                        ww = [z_] if slot == 0 else []
                        aa = [z_] if slot else []
                        if eng == "scalar":
                            op("scalar", lambda e, z_=z_, pr=pr: e.copy(out=z_[pr, :, slot, :], in_=pv[pr, :, :]), r=[bk], w=ww, a=aa)
                        else:
                            op("vector", lambda e, z_=z_, pr=pr: e.tensor_copy(out=z_[pr, :, slot, :], in_=pv[pr, :, :]), r=[bk], w=ww, a=aa)
                    return
                ww = [dst] if slot == 0 else []
                aa = [dst] if slot else []
                if eng == "scalar":
                    op("scalar", lambda e: e.copy(out=dst[:, :, slot, :], in_=pv), r=[bk], w=ww, a=aa)
                else:
                    op("vector", lambda e: e.tensor_copy(out=dst[:, :, slot, :], in_=pv), r=[bk], w=ww, a=aa)
            kb.transpose8(ev, src)
        dump(8, BKT[:, :, 0, :] if False else Kt)
        Yt = F[0]
        op("gpsimd", lambda e: e.memset(Yt[:, 0:1], 0.0), w=[Yt])

        def hgroup(hg):
            s_ = hg % 2
            MI, MK, MT, Nk, NkT, Tk, G, Stmp = MIs[s_], MKs[s_], MTs[s_], Nks[s_], NkTs[s_], Tks[s_], Gs[s_], Stmps[s_]
            WT, Ab, RbT, Pp, St, Sb = WTh[hg], Abh[hg], RbTh[hg], Pph[hg], Sth[hg], Sbh[hg]
            bkY = kb.ps[5 + s_]
            heads = [4 * hg + i for i in range(4)]
            for which, dst, mask in ((0, MI, kb.mask_si), (1, MK, kb.mask_si), (2, MT, kb.mask_tt)):
                dst5 = dst[:, :, :, :].rearrange("p (pl par) a t -> p pl par a t", par=2)
                for ip, par in enumerate((0, 1) if which % 2 == 0 else (1, 0)):
                    bk = kb.bank()
                    for pl in range(2):
                        h = heads[2 * pl + par]
                        p, b0 = h // 2, 64 * (h % 2)
                        z_ = ARTz[par]
                        if which == 0:
                            l, rr_ = BKT[:, p, 0, :], z_[:, p, :, :]
                        elif which == 1:
                            l, rr_ = BKT[:, p, 1, :], z_[:, p, :, :]
                        else:
                            l, rr_ = z_[:, p, 0, :], BKT[:, p, :, :]
                        op("tensor", lambda e, bk=bk, l=l, rr_=rr_, pl=pl: e.matmul(
                            bk[:, 256 * pl:256 * pl + 256], lhsT=l, rhs=rr_.rearrange("p a t -> p (a t)"),
                            start=True, stop=True), r=[z_, BKT], w=[bk] if pl == 0 else [], a=[bk] if pl else [])
                    op("vector", lambda e, bk=bk, dst5=dst5, mask=mask, par=par: e.tensor_tensor(
                        out=dst5[:, :, par, :, :].rearrange("p pl a t -> p pl (a t)"),
                        in0=bk[:, :].rearrange("p (pl x) -> p pl x", pl=2),
                        in1=mask[:, :].rearrange("p (pl x) -> p pl x", pl=2),
                        op=ALU.mult), r=[bk, mask], w=[dst] if ip == 0 else [], a=[dst] if ip else [])
                yield
            identb = kb.ident[:, :].unsqueeze(1).broadcast_to([128, 4, 128])
            op("vector", lambda e: e.tensor_tensor(out=Tk[0][:, :, :], in0=MI[:, :, 0, :], in1=identb, op=ALU.add),
               r=[MI, kb.ident], w=[Tk[0]])
            curN = lambda i: MI[:, i, 0, :]
            curNT = lambda i: MT[:, i, 0, :]
            curN_t, curNT_t = MI, MT
            for lv in range(1, 6):
                nN, nNT = Nk[lv % 2], NkT[lv % 2]
                if lv < 5:
                    bk = kb.bank()
                    for i in range(4):
                        op("tensor", lambda e, bk=bk, i=i, a_=curNT(i), b_=curN(i): e.matmul(
                            bk[:, 128 * i:128 * i + 128], lhsT=a_, rhs=b_, start=True, stop=True),
                           r=[curN_t, curNT_t], w=[bk] if i == 0 else [], a=[bk] if i else [])
                    op("scalar", lambda e, bk=bk, nN=nN: e.copy(out=nN[:, :, :].rearrange("p h t -> p (h t)"), in_=bk[:, :]),
                       r=[bk], w=[nN])
                bk2 = kb.bank()
                for i in range(4):
                    op("tensor", lambda e, bk2=bk2, i=i, a_=curN(i), b_=curNT(i): e.matmul(
                        bk2[:, 128 * i:128 * i + 128], lhsT=a_, rhs=b_, start=True, stop=True),
                       r=[curN_t, curNT_t], w=[bk2] if i == 0 else [], a=[bk2] if i else [])
                op("scalar", lambda e, bk2=bk2, nNT=nNT: e.copy(out=nNT[:, :, :].rearrange("p h t -> p (h t)"), in_=bk2[:, :]),
                   r=[bk2], w=[nNT])
                yield
                To, Tn = Tk[(lv - 1) % 2], Tk[lv % 2]
                bk3 = kb.bank()
                for i in range(4):
                    op("tensor", lambda e, bk3=bk3, i=i, nNT=nNT, To=To: e.matmul(
                        bk3[:, 128 * i:128 * i + 128], lhsT=nNT[:, i, :], rhs=To[:, i, :], start=True, stop=True),
                       r=[nNT, To], w=[bk3] if i == 0 else [], a=[bk3] if i else [])
                op("vector", lambda e, bk3=bk3, To=To, Tn=Tn: e.tensor_tensor(
                    out=Tn[:, :, :].rearrange("p h t -> p (h t)"), in0=bk3[:, :],
                    in1=To[:, :, :].rearrange("p h t -> p (h t)"), op=ALU.add), r=[bk3, To], w=[Tn])
                yield
                curN = lambda i, nN=nN: nN[:, i, :]
                curNT = lambda i, nNT=nNT: nNT[:, i, :]
                curN_t, curNT_t = nN, nNT
            Tf = Tk[5 % 2]
            bk = kb.bank()
            for i in range(4):
                op("tensor", lambda e, bk=bk, i=i: e.matmul(bk[:, 128 * i:128 * i + 128], lhsT=MT[:, i, 1, :], rhs=Tf[:, i, :],
                                                            start=True, stop=True),
                   r=[MT, Tf], w=[bk] if i == 0 else [], a=[bk] if i else [])
            op("scalar", lambda e, bk=bk: e.copy(out=G[:, :, :].rearrange("p h t -> p (h t)"), in_=bk[:, :]), r=[bk], w=[G])
            yield
            bk = kb.bank()
            for i in range(4):
                h = heads[i]
                op("tensor", lambda e, bk=bk, i=i, h=h: e.matmul(bk[:, 64 * i:64 * i + 64], lhsT=G[:, i, :],
                                                                 rhs=Vb[:, 64 * h:64 * h + 64], start=True, stop=True),
                   r=[G, Vb], w=[bk] if i == 0 else [], a=[bk] if i else [])
                op("tensor", lambda e, bk=bk, i=i, h=h: e.matmul(bk[:, 256 + 64 * i:256 + 64 * i + 64], lhsT=Tf[:, i, :],
                                                                 rhs=At[:, 64 * h:64 * h + 64], start=True, stop=True),
                   r=[Tf, At], a=[bk])
            hs = slice(256 * hg, 256 * hg + 256)
            op("scalar", lambda e, bk=bk: e.copy(out=WT[:, :], in_=bk[:, 0:256]), r=[bk], w=[WT])
            op("scalar", lambda e, bk=bk: e.copy(out=Ab[:, :], in_=bk[:, 256:512]), r=[bk], w=[Ab])
            yield
            bk = kb.bank()
            for i in range(4):
                h = heads[i]
                b0 = 64 * (h % 2)
                pl = i // 2
                op("tensor", lambda e, bk=bk, i=i, b0=b0, pl=pl: e.matmul(
                    bk[b0:b0 + 64, 128 * pl:128 * pl + 128], lhsT=Ab[:, 64 * i:64 * i + 64], rhs=MI[:, i, 1, :],
                    start=True, stop=True), r=[Ab, MI], w=[bk] if i == 0 else [], a=[bk] if i else [])
            for par in range(2):
                pr = slice(64 * par, 64 * par + 64)
                z_ = ARTz[par]
                op("vector", lambda e, bk=bk, pr=pr, z_=z_: e.tensor_tensor(
                    out=RbT[pr, :, :], in0=bk[pr, 0:256].rearrange("p (a t) -> p a t", t=128),
                    in1=z_[pr, 2 * hg:2 * hg + 2, 1, :], op=ALU.add), r=[bk, z_],
                   w=[RbT] if par == 0 else [], a=[RbT] if par else [])
            bk = kb.bank()
            firstP = True
            for ch in range(2):
                for i in range(4):
                    h = heads[i]
                    b0 = 64 * (h % 2)
                    pl = i // 2
                    col = (pl * 2 + ch) * 64
                    op("tensor", lambda e, bk=bk, h=h, i=i, b0=b0, ch=ch, col=col: e.matmul(
                        bk[b0:b0 + 64, col:col + 64], lhsT=Ab[64 * ch:64 * ch + 64, 64 * i:64 * i + 64],
                        rhs=Bg[64 * ch:64 * ch + 64, 64 * h:64 * h + 64], start=True, stop=True),
                       r=[Ab, Bg], w=[bk] if firstP else [], a=[] if firstP else [bk])
                    firstP = False
            op("scalar", lambda e, bk=bk: e.copy(
                out=Pp[:, :, :, :].rearrange("p a c j -> p (a c j)"), in_=bk[:, 0:256]), r=[bk], w=[Pp])
            yield
            firstY = True
            for i in range(4):
                h = heads[i]
                for lhs_t, rhs_t, rhs_ap in ((MI, WT, WT[:, 64 * i:64 * i + 64]), (MK, Vb, Vb[:, 64 * h:64 * h + 64])):
                    op("tensor", lambda e, i=i, lhs_t=lhs_t, rhs_ap=rhs_ap, firstY=firstY: e.matmul(
                        bkY[:, 64 * i:64 * i + 64], lhsT=lhs_t[:, i, 1, :], rhs=rhs_ap,
                        start=firstY, stop=False, skip_group_check=True), r=[lhs_t, rhs_t],
                       w=[bkY] if firstY else [], a=[] if firstY else [bkY])
                    firstY = False
            yield
            for ch in range(2):
                for i in (0, 2, 1, 3):
                    h = heads[i]
                    pl, b0 = i // 2, 64 * (h % 2)
                    op("tensor", lambda e, i=i, pl=pl, b0=b0, ch=ch: e.matmul(
                        bkY[64 * ch:64 * ch + 64, 64 * i:64 * i + 64], lhsT=RbT[b0:b0 + 64, pl, 64 * ch:64 * ch + 64],
                        rhs=Sb[b0:b0 + 64, pl, :], start=False, stop=(ch == 1), skip_group_check=True),
                       r=[RbT, Sb], a=[bkY])
                bkS = kb.bank()
                seen_par = set()
                firstS = True
                outvs = {}
                for i in (0, 2, 1, 3):
                    h = heads[i]
                    pl, b0 = i // 2, 64 * (h % 2)
                    outv = bkS[b0:b0 + 64, 64 * pl:64 * pl + 64]
                    outvs[i] = outv
                    st_ = (b0 not in seen_par)
                    seen_par.add(b0)
                    op("tensor", lambda e, outv=outv, pl=pl, b0=b0, ch=ch, st_=st_: e.matmul(
                        outv, lhsT=Pp[b0:b0 + 64, pl, ch, :], rhs=St[b0:b0 + 64, pl, :], start=st_, stop=False,
                        skip_group_check=True), r=[Pp, St], w=[bkS] if firstS else [], a=[] if firstS else [bkS])
                    firstS = False
                for i in range(4):
                    h = heads[i]
                    outv = outvs[i]
                    op("tensor", lambda e, outv=outv, h=h, i=i, ch=ch: e.matmul(
                        outv, lhsT=Bg[64 * ch:64 * ch + 64, 64 * h:64 * h + 64], rhs=WT[64 * ch:64 * ch + 64, 64 * i:64 * i + 64],
                        start=False, stop=False, skip_group_check=True), r=[Bg, WT], a=[bkS])
                    op("tensor", lambda e, outv=outv, h=h, ch=ch: e.matmul(
                        outv, lhsT=Kg[64 * ch:64 * ch + 64, 64 * h:64 * h + 64], rhs=Vb[64 * ch:64 * ch + 64, 64 * h:64 * h + 64],
                        start=False, stop=True, skip_group_check=True), r=[Kg, Vb], a=[bkS])
                op("vector", lambda e, ch=ch: e.tensor_tensor(
                    out=Stmp[:, :, :], in0=St[:, :, :], in1=gCc[:, 2 * hg:2 * hg + 2, ch:ch + 1].broadcast_to([128, 2, 64]),
                    op=ALU.mult), r=[St, gCc], w=[Stmp])
                op("vector", lambda e, bkS=bkS: e.tensor_tensor(
                    out=St[:, :, :], in0=bkS[:, 0:128].rearrange("p (a i) -> p a i", i=64), in1=Stmp[:, :, :],
                    op=ALU.add), r=[bkS, Stmp], w=[St])
                op("scalar", lambda e: e.copy(out=Sb[:, :, :], in_=St[:, :, :]), r=[St], w=[Sb])
                yield
            op("scalar", lambda e, hs=hs: e.copy(out=Yt[:, hs], in_=bkY[:, 0:256]), r=[bkY], a=[Yt])
            yield

        def chain2(a_, b_):
            yield from a_
            yield from b_
        gens = [chain2(hgroup(0), hgroup(2)), chain2(hgroup(1), hgroup(3))]
        if it + 1 < NT:
            gens.append(part1_g(it + 1))
        while gens:
            for g_ in list(gens):
                try:
                    next(g_)
                except StopIteration:
                    gens.remove(g_)
        dump(11, Yt, True)
        sq = F[1]
        op("vector", lambda e: e.tensor_reduce(out=sm[2][:, :], in_=h3(Yt), axis=AX.X, op=ALU.add), r=[Yt], w=[sm[2]])
        op("gpsimd", lambda e: e.tensor_tensor(out=sq[:, :], in0=Yt[:, :], in1=Yt[:, :], op=ALU.mult), r=[Yt], w=[sq])
        op("vector", lambda e: e.tensor_reduce(out=sm[3][:, :], in_=h3(sq), axis=AX.X, op=ALU.add), r=[sq], w=[sm[3]])
        mean, var = sm[2], sm[3]
        op("vector", lambda e: e.tensor_scalar_mul(out=mean[:, :], in0=mean[:, :], scalar1=1.0 / 64), r=[mean], w=[mean])
        op("vector", lambda e: e.tensor_tensor(out=sm[4][:, :], in0=mean[:, :], in1=mean[:, :], op=ALU.mult), r=[mean], w=[sm[4]])
        op("vector", lambda e: e.scalar_tensor_tensor(out=var[:, :], in0=var[:, :], scalar=1.0 / 64, in1=sm[4][:, :],
                                                      op0=ALU.mult, op1=ALU.subtract), r=[var, sm[4]], w=[var])
        op("scalar", lambda e: e.activation(out=var[:, :], in_=var[:, :], func=AF.Sqrt, bias=GN_EPS, scale=1.0), r=[var], w=[var])
        op("vector", lambda e: e.reciprocal(out=var[:, :], in_=var[:, :]), r=[var], w=[var])
        yn = F[1]
        op("vector", lambda e: e.tensor_tensor(out=h3(yn), in0=h3(Yt), in1=bc3(mean), op=ALU.subtract), r=[Yt, mean], w=[yn])
        op("vector", lambda e: e.tensor_tensor(out=h3(yn), in0=h3(yn), in1=bc3(var), op=ALU.mult), r=[yn, var], w=[yn])
        op("vector", lambda e: e.tensor_tensor(out=yn[:, :], in0=yn[:, :], in1=bc_lg[:, :], op=ALU.mult), r=[yn, bc_lg], w=[yn])
        op("vector", lambda e: e.tensor_tensor(out=yn[:, :], in0=yn[:, :], in1=bc_lb[:, :], op=ALU.add), r=[yn, bc_lb], w=[yn])
        bv = F[3]
        op("vector", lambda e: e.tensor_tensor(out=h3(bv), in0=h3(Vb), in1=bc3(bsum), op=ALU.mult), r=[Vb, bsum], w=[bv])
        op("vector", lambda e: e.tensor_tensor(out=yn[:, :], in0=yn[:, :], in1=bv[:, :], op=ALU.add), r=[yn, bv], w=[yn])
        op("vector", lambda e: e.tensor_tensor(out=ygt[:, :], in0=yn[:, :], in1=sgb[:, :], op=ALU.mult), r=[yn, sgb], w=[ygt])
        return kb.dma("sync", yg_out[it * 128:(it + 1) * 128, :], ygt[:, :], r=[ygt], key="styg")
    last = None
    try:
        for it_ in range(NT):
            last = tile(it_)
    except Stop:
        last = kb.last
    kb.mem.reset(m0)
    kb.rot = [0, 1, 2, 3, 4, 5, 6]
    return last


LAM_INIT = 0.8 - 0.6 * math.exp(-0.3 * 1)


class TV:
    def __init__(self, ap, buf):
        self.ap = ap
        self.buf = buf

    def __getitem__(self, k):
        return self.ap[k]


class PM:
    def __init__(self, kb, dram, l, wout_name, stg=None):
        al = kb.alloc
        self.kb = kb
        self.dram, self.l, self.wout_name = dram, l, wout_name
        self.Wout = al("Wout", [128, 8, 1024], BF16)
        self.Wg = al("Wg", [128, 8, 1024], BF16)
        self.Wp = al("Wp", [128, 2, 1024], BF16)
        self.bc_post = al("bc_post", [128, 1024], BF16)
        self.bc_ple = al("bc_ple", [128, 1024], BF16)
        if stg is not None:
            self.load(stg)

    def load(self, stg):
        kb, dram, l, wout_name = self.kb, self.dram, self.l, self.wout_name
        kb.load_w_bf16(self.Wout, dram[wout_name], 1024, stg)
        kb.load_w_bf16(self.Wg, dram["ple_gate"][l], 1024, stg)
        kb.load_w_bf16(self.Wp, dram["ple_w"][l], 1024, stg, nck=2)
        kb.load_bcast(self.bc_post, dram["norm_post"][l], stg=stg[0], key="bcA")
        kb.load_bcast(self.bc_ple, dram["ple_norm"][l], stg=stg[1], key="bcB")

    def alloc_tmps(self):
        al = self.kb.alloc

        class TM:
            pass
        tm = TM()
        tm.ygT = al("ygT", [128, 8, 128], BF16)
        tm.ot = al("ot", [128, 1024], F32)
        tm.tmp = al("pm_tmp", [128, 1024], F32)
        tm.hmid = al("hmid", [128, 1024], F32)
        tm.pb = al("pb", [128, 256], BF16)
        tm.pT = al("pT", [128, 2, 128], BF16)
        tm.e_sb = al("e_sb", [128, 1024], F32)
        tm.hb = al("hb", [128, 1024], BF16)
        tm.hT = tm.ygT
        tm.ge = al("ge", [128, 1024], F32)
        tm.st = al("pm_st", [128, 4], F32)
        tm.junk = TV(tm.tmp.h[:, :].bitcast(BF16)[:, 0:1024], tm.tmp.buf)
        return tm

    def rstd_from(self, tm, srcT, src, col, eps, scale_sq, n=1024):
        kb, st, junk = self.kb, tm.st, tm.junk
        kb.op("gpsimd", lambda e: e.memset(st[:, col:col + 1], 0.0), w=[st])
        kb.op("scalar", lambda e: e.activation(out=junk[:, 0:n], in_=src, func=AF.Square, scale=scale_sq,
                                               accum_out=st[:, col:col + 1]), r=[srcT, st], w=[junk, st])
        kb.op("scalar", lambda e: e.activation(out=st[:, col:col + 1], in_=st[:, col:col + 1], func=AF.Sqrt, bias=eps,
                                               scale=1.0), r=[st], w=[st])
        kb.op("vector", lambda e: e.reciprocal(out=st[:, col:col + 1], in_=st[:, col:col + 1]), r=[st], w=[st])

    def run(self, ygt, resid, pt, hout, tm):
        for _ in self.run_g(ygt, resid, pt, hout, tm):
            pass

    def run_g(self, ygt, resid, pt, hout, tm):
        kb = self.kb
        op = kb.op
        ygT, ot, tmp, hmid, pb, pT, e_sb, hb, hT, ge, st = (tm.ygT, tm.ot, tm.tmp, tm.hmid, tm.pb, tm.pT,
                                                            tm.e_sb, tm.hb, tm.hT, tm.ge, tm.st)
        kb.transpose8(lambda bk, pv: op("scalar", lambda e: e.copy(out=ygT[:, :, :], in_=pv), r=[bk], w=[ygT]), ygt)
        yield
        for half in range(2):
            bk = kb.bank()
            cs = slice(half * 512, half * 512 + 512)
            for ck in range(8):
                op("tensor", lambda e, bk=bk, ck=ck, cs=cs: e.matmul(bk[:, :], lhsT=ygT[:, ck, :], rhs=self.Wout[:, ck, cs],
                                                                     start=(ck == 0), stop=(ck == 7)),
                   r=[ygT, self.Wout], w=[bk] if ck == 0 else [], a=[bk] if ck else [])
            op("vector", lambda e, bk=bk, cs=cs: e.tensor_copy(out=ot[:, cs], in_=bk[:, :]), r=[bk],
               w=[ot] if half == 0 else [], a=[ot] if half else [])
        yield
        self.rstd_from(tm, ot, ot[:, :], 0, NORM_EPS, 1.0 / 32)
        op("gpsimd", lambda e: e.tensor_copy(out=pb[:, :], in_=pt[:, :]), r=[pt], w=[pb])
        kb.transpose8(lambda bk, pv: op("scalar", lambda e: e.copy(out=pT[:, :, :], in_=pv[:, 0:2, :]), r=[bk], w=[pT]),
                      pb, nck=2)
        yield
        op("vector", lambda e: e.scalar_tensor_tensor(out=tmp[:, :], in0=ot[:, :], scalar=st[:, 0:1], in1=self.bc_post[:, :],
                                                      op0=ALU.mult, op1=ALU.mult), r=[ot, st, self.bc_post], w=[tmp])
        op("vector", lambda e: e.tensor_tensor(out=hmid[:, :], in0=tmp[:, :], in1=resid[:, :], op=ALU.add),
           r=[tmp, resid], w=[hmid])
        for half in range(2):
            bk = kb.bank()
            cs = slice(half * 512, half * 512 + 512)
            for ck in range(2):
                op("tensor", lambda e, bk=bk, ck=ck, cs=cs: e.matmul(bk[:, :], lhsT=pT[:, ck, :], rhs=self.Wp[:, ck, cs],
                                                                     start=(ck == 0), stop=(ck == 1)),
                   r=[pT, self.Wp], w=[bk] if ck == 0 else [], a=[bk] if ck else [])
            op("scalar", lambda e, bk=bk, cs=cs: e.copy(out=e_sb[:, cs], in_=bk[:, :]), r=[bk],
               w=[e_sb] if half == 0 else [], a=[e_sb] if half else [])
        yield
        op("scalar", lambda e: e.copy(out=hb[:, :], in_=hmid[:, :]), r=[hmid], w=[hb])
        kb.transpose8(lambda bk, pv: op("vector", lambda e: e.tensor_copy(out=hT[:, :, :], in_=pv), r=[bk], w=[hT]), hb)
        yield
        for half in range(2):
            bk = kb.bank()
            cs = slice(half * 512, half * 512 + 512)
            for ck in range(8):
                op("tensor", lambda e, bk=bk, ck=ck, cs=cs: e.matmul(bk[:, :], lhsT=hT[:, ck, :], rhs=self.Wg[:, ck, cs],
                                                                     start=(ck == 0), stop=(ck == 7)),
                   r=[hT, self.Wg], w=[bk] if ck == 0 else [], a=[bk] if ck else [])
            op("scalar", lambda e, bk=bk, cs=cs: e.activation(out=ge[:, cs], in_=bk[:, :], func=AF.Sigmoid), r=[bk],
               w=[ge] if half == 0 else [], a=[ge] if half else [])
        yield
        op("vector", lambda e: e.tensor_tensor(out=ge[:, :], in0=ge[:, :], in1=e_sb[:, :], op=ALU.mult), r=[ge, e_sb], w=[ge])
        self.rstd_from(tm, ge, ge[:, :], 1, NORM_EPS, 1.0 / 32)
        yield
        op("vector", lambda e: e.scalar_tensor_tensor(out=tmp[:, :], in0=ge[:, :], scalar=st[:, 1:2], in1=self.bc_ple[:, :],
                                                      op0=ALU.mult, op1=ALU.mult), r=[ge, st, self.bc_ple], w=[tmp])
        op("vector", lambda e: e.tensor_tensor(out=hout[:, :], in0=tmp[:, :], in1=hmid[:, :], op=ALU.add),
           r=[tmp, hmid], w=[hout])
        yield


def interleave(gens):
    gens = list(gens)
    while gens:
        for g_ in list(gens):
            try:
                next(g_)
            except StopIteration:
                gens.remove(g_)


def rope_ops(kb, src, dst, cs_t, tmp, it):
    op = kb.op
    v3 = lambda t: t[:, :].rearrange("p (g d) -> p g d", d=64)
    cosb = cs_t[:, 0, :].unsqueeze(1).broadcast_to([128, 16, 8])
    sinb = cs_t[:, 1, :].unsqueeze(1).broadcast_to([128, 16, 8])
    t4 = tmp[:, 0:512].rearrange("p (k g d) -> p k g d", k=4, d=8)
    op("scalar", lambda e: e.copy(out=dst[:, :], in_=src[:, :]), r=[src], w=[dst])
    x1, x2 = v3(src)[:, :, 0:8], v3(src)[:, :, 8:16]
    op("vector", lambda e: e.tensor_tensor(out=t4[:, 0], in0=x1, in1=cosb, op=ALU.mult), r=[src, cs_t], w=[tmp])
    op("vector", lambda e: e.tensor_tensor(out=t4[:, 1], in0=x2, in1=sinb, op=ALU.mult), r=[src, cs_t], a=[tmp])
    op("vector", lambda e: e.tensor_tensor(out=t4[:, 2], in0=x2, in1=cosb, op=ALU.mult), r=[src, cs_t], a=[tmp])
    op("vector", lambda e: e.tensor_tensor(out=t4[:, 3], in0=x1, in1=sinb, op=ALU.mult), r=[src, cs_t], a=[tmp])
    op("vector", lambda e: e.tensor_tensor(out=v3(dst)[:, :, 0:8], in0=t4[:, 0], in1=t4[:, 1], op=ALU.subtract),
       r=[tmp], a=[dst])
    op("vector", lambda e: e.tensor_tensor(out=v3(dst)[:, :, 8:16], in0=t4[:, 2], in1=t4[:, 3], op=ALU.add),
       r=[tmp], a=[dst])


def phase_a2(kb, dram, yg_in, h1_out, kt_out, v_out):
    nc, S = kb.nc, kb.S
    al, op = kb.alloc, kb.op
    NT = S // 128
    kb.c.barrier()
    m0 = kb.mem.mark()
    Wkv = al("Wkv", [128, 8, 2048], BF16)
    gkv_fm = al("gkv_fm", [128, 1, 8], F32)
    pm = PM(kb, dram, 0, "a_w_out")
    mstg = kb.mem.mark()
    stg = [al(f"wstgB{i}", [128, 1024], F32) for i in range(3)]
    pm.load(stg)
    kb.load_w_bf16(Wkv, dram["kv_w"], 2048, stg)
    kb.load_fm_vecs(gkv_fm, [dram["kv_norm"]], stg[2])
    kb.c.barrier()
    kb.mem.reset(mstg)
    tms = [pm.alloc_tmps() for _ in range(2)]
    NI = 4
    xt = [al(f"xtA{i}", [128, 1024], F32) for i in range(NI)]
    ygt = [al(f"ygtA{i}", [128, 1024], BF16) for i in range(NI)]
    ptl = [al(f"ptA{i}", [128, 256], F32) for i in range(NI)]
    cst = [al(f"cst{i}", [128, 2, 8], F32) for i in range(NI)]
    h1s = [al(f"h1_{i}", [128, 1024], F32) for i in range(2)]
    Kbs = [al(f"Kb{i}", [128, 1024], BF16) for i in range(2)]
    KTss = [al(f"KTsb{i}", [128, 8, 128], BF16) for i in range(2)]
    Vts = [al(f"Vt{i}", [128, 1024], BF16) for i in range(2)]
    print("A2 sbuf peak", kb.mem.peak)

    def loads(it):
        b = it % NI
        sl = slice(it * 128, (it + 1) * 128)
        kb.dma("sync", xt[b][:, :], dram["x"][sl, :], w=[xt[b]], key=f"a2x{b}")
        kb.dma("sync", ygt[b][:, :], yg_in[sl, :], w=[ygt[b]], key=f"a2y{b}")
        kb.dma("sync", ptl[b][:, :], dram["p"][0, sl, :], w=[ptl[b]], key=f"a2p{b}")
        kb.dma("sync", cst[b][:, 0, :], dram["c_cos"][sl, :], w=[cst[b]], key=f"a2c{b}")
        kb.dma("sync", cst[b][:, 1, :], dram["c_sin"][sl, :], a=[cst[b]], key=f"a2c{b}")

    lasts = []

    def tile_g(it, P):
        b = it % NI
        sl = slice(it * 128, (it + 1) * 128)
        tm = tms[P]
        h1, Kb, KTs, Vt = h1s[P], Kbs[P], KTss[P], Vts[P]
        hk, hkT, Kf, rtmp = tm.hb, tm.hT, tm.ot, tm.tmp
        yield from pm.run_g(ygt[b], xt[b], ptl[b], h1, tm)
        kb.dma("sync", h1_out[sl, :], h1[:, :], r=[h1], key=f"sth1_{P}")
        pm.rstd_from(tm, h1, h1[:, :], 2, NORM_EPS, 1.0 / 32)
        yield
        op("vector", lambda e: e.tensor_scalar(out=hk[:, :], in0=h1[:, :], scalar1=tm.st[:, 2:3], scalar2=None, op0=ALU.mult),
           r=[h1, tm.st], w=[hk])
        kb.transpose8(lambda bk, pv: op("vector", lambda e: e.tensor_tensor(
            out=hkT[:, :, :], in0=pv, in1=gkv_fm[:, 0, :].unsqueeze(2).broadcast_to([128, 8, 128]), op=ALU.mult),
            r=[bk, gkv_fm], w=[hkT]), hk)
        yield
        for which in range(2):
            for half in range(2):
                bk = kb.bank()
                c0 = which * 1024 + half * 512
                cs = slice(half * 512, half * 512 + 512)
                for ck in range(8):
                    op("tensor", lambda e, bk=bk, ck=ck, c0=c0: e.matmul(bk[:, :], lhsT=hkT[:, ck, :], rhs=Wkv[:, ck, c0:c0 + 512],
                                                                         start=(ck == 0), stop=(ck == 7)),
                       r=[hkT, Wkv], w=[bk] if ck == 0 else [], a=[bk] if ck else [])
                dstt = Kf if which == 0 else Vt
                op("scalar", lambda e, bk=bk, cs=cs, dstt=dstt: e.copy(out=dstt[:, cs], in_=bk[:, :]), r=[bk],
                   w=[dstt] if half == 0 else [], a=[dstt] if half else [])
            yield
        rope_ops(kb, Kf, Kb, cst[b], rtmp, it)
        yield
        kb.transpose8(lambda bk, pv: op("scalar", lambda e: e.copy(out=KTs[:, :, :], in_=pv), r=[bk], w=[KTs]), Kb)
        kb.dma("sync", kt_out[:, :, sl].rearrange("h p t -> p h t"), KTs[:, :, :], r=[KTs], key=f"stkt_{P}")
        lasts.append(kb.dma("sync", v_out[sl, :], Vt[:, :], r=[Vt], key=f"stv_{P}"))
        yield

    loads(0)
    if NT > 1:
        loads(1)
    for it in range(0, NT, 2):
        for nx in (it + 2, it + 3):
            if nx < NT:
                loads(nx)
        gens = [tile_g(it, 0)]
        if it + 1 < NT:
            gens.append(tile_g(it + 1, 1))
        interleave(gens)
    kb.mem.reset(m0)
    return lasts[-1]


def phase_b(kb, dram, h1_in, kt_in, v_in, out):
    nc, S = kb.nc, kb.S
    al, op = kb.alloc, kb.op
    NG = S // 256
    CH = 8
    kb.c.barrier()
    m0 = kb.mem.mark()
    Win = al("WinB", [128, 8, 2048], BF16)
    g1_fm = al("g1_fm", [128, 1, 8], F32)
    bc_sub = al("bc_sub", [128, 128], F32)
    lq = al("lq", [128, 256], F32)
    lam = al("lam", [128, 4], F32)
    pm = PM(kb, dram, 1, "b_w_out")
    mstg = kb.mem.mark()
    stg = [al(f"wstgC{i}", [128, 1024], F32) for i in range(3)]
    pm.load(stg)
    kb.load_w_bf16(Win, dram["b_w_in"], 2048, stg)
    kb.load_fm_vecs(g1_fm, [dram["norm_pre"][1]], stg[2])
    kb.dma("sync", bc_sub[:, :], dram["b_subln"].partition_broadcast(128), w=[bc_sub], key="bsub")
    op("vector", lambda e: e.tensor_scalar_mul(out=bc_sub[:, :], in0=bc_sub[:, :], scalar1=1.0 - LAM_INIT), r=[bc_sub], w=[bc_sub])
    kb.dma("sync", lq[:, :], dram["b_lambda"].rearrange("a b -> (a b)").partition_broadcast(128), w=[lq], key="blam")
    lq3 = lq[:, :].rearrange("p (a b) -> p a b", b=64)
    op("vector", lambda e: e.tensor_tensor(out=lq3[:, 0, :], in0=lq3[:, 0, :], in1=lq3[:, 1, :], op=ALU.mult), r=[lq], w=[lq])
    op("vector", lambda e: e.tensor_tensor(out=lq3[:, 2, :], in0=lq3[:, 2, :], in1=lq3[:, 3, :], op=ALU.mult), r=[lq], w=[lq])
    op("vector", lambda e: e.tensor_reduce(out=lam[:, 0:1], in_=lq3[:, 0, :], axis=AX.X, op=ALU.add), r=[lq], w=[lam])
    op("vector", lambda e: e.tensor_reduce(out=lam[:, 1:2], in_=lq3[:, 2, :], axis=AX.X, op=ALU.add), r=[lq, lam], w=[lam])
    op("scalar", lambda e: e.activation(out=lam[:, 0:2], in_=lam[:, 0:2], func=AF.Exp), r=[lam], w=[lam])
    op("vector", lambda e: e.tensor_tensor(out=lam[:, 2:3], in0=lam[:, 1:2], in1=lam[:, 0:1], op=ALU.subtract), r=[lam], w=[lam])
    op("vector", lambda e: e.tensor_scalar_add(out=lam[:, 3:4], in0=lam[:, 2:3], scalar1=-LAM_INIT), r=[lam], w=[lam])
    neglam = lam[:, 3:4]
    kb.c.barrier()
    kb.mem.reset(mstg)
    tmB = pm.alloc_tmps()
    h1t = [[al(f"h1t{i}_{j}", [128, 1024], F32) for j in range(2)] for i in range(3)]
    ptl = [[al(f"ptB{i}_{j}", [128, 256], F32) for j in range(2)] for i in range(3)]
    cst = [[al(f"cstB{i}_{j}", [128, 2, 8], F32) for j in range(2)] for i in range(3)]
    xs = tmB.junk
    xnT = al("xnTB", [128, 8, 128], BF16)
    Qb = tmB.hb
    QTp = [al(f"QTp{i}", [128, 8, 2, 256], BF16) for i in range(2)]
    sgb = [[al(f"sgbB{i}_{j}", [128, 1024], BF16) for j in range(2)] for i in range(2)]
    og = [[al(f"og{i}_{j}", [128, 1024], BF16) for j in range(2)] for i in range(2)]
    PT = [al(f"PT{i}", [128, 2, 256], BF16) for i in range(3)]
    NB = 3
    KTc = [al(f"KTc{i}", [128, CH * 128], BF16) for i in range(NB)]
    Vc = [al(f"Vc{i}", [128, CH, 129], BF16) for i in range(NB)]
    t02 = [al(f"t0_{i}", [128, 128], F32) for i in range(2)]
    ob2 = [al(f"ob_{i}", [128, 128], F32) for i in range(2)]
    rr2 = [al(f"rr_{i}", [128, 4], F32) for i in range(2)]
    ast = al("ast", [128, 4], F32)
    ajunk = al("ajunk", [128, 128], BF16)
    hout = al("houtB", [128, 1024], F32)
    rtmp = tmB.tmp
    Qf = tmB.ot
    print("B sbuf peak", kb.mem.peak)
    for V_ in Vc:
        op("vector", lambda e, V_=V_: e.memset(V_[:, :, 128:129], 1.0), w=[V_])
    for Q_ in QTp:
        op("gpsimd", lambda e, Q_=Q_: e.memset(Q_[:, :, :, :], 0.0), w=[Q_])
    kb.rot = [7]
    kb.rr = 0
    abanks = [kb.ps[4], kb.ps[5], kb.ps[6]]
    acnt = [0]
    acc = [[kb.ps[2 * qt + c] for c in range(2)] for qt in range(2)]
    st = tmB.st

    chunks = []
    for g in range(NG):
        n = 2 * g + 2
        for h in range(8):
            for c0 in range(0, n, CH):
                chunks.append((g, h, c0, min(CH, n - c0)))
    cidx = {(g, h, c0): i for i, (g, h, c0, m) in enumerate(chunks)}
    issued = [0]

    def issue_chunks(upto):
        while issued[0] < min(upto, len(chunks)):
            i = issued[0]
            g_, h_, c0, m = chunks[i]
            b = i % NB
            kb.dma("sync", KTc[b][:, 0:m * 128], kt_in[h_, :, c0 * 128:(c0 + m) * 128], w=[KTc[b]], key=f"ldk{b}")
            kb.dma("sync", Vc[b][:, 0:m, 0:128],
                   v_in[c0 * 128:(c0 + m) * 128, h_ * 128:(h_ + 1) * 128].rearrange("(n p) d -> p n d", p=128),
                   w=[Vc[b]], key=f"ldv{b}")
            issued[0] += 1

    def loads(g):
        s3 = g % 3
        for j in range(2):
            it = 2 * g + j
            sl = slice(it * 128, (it + 1) * 128)
            kb.dma("sync", h1t[s3][j][:, :], h1_in[sl, :], w=[h1t[s3][j]], key=f"bh{s3}{j}")
            kb.dma("sync", ptl[s3][j][:, :], dram["p"][1, sl, :], w=[ptl[s3][j]], key=f"bp{s3}{j}")
            kb.dma("sync", cst[s3][j][:, 0, :], dram["c_cos"][sl, :], w=[cst[s3][j]], key=f"bc{s3}{j}")
            kb.dma("sync", cst[s3][j][:, 1, :], dram["c_sin"][sl, :], a=[cst[s3][j]], key=f"bc{s3}{j}")

    def pre_g(g):
        s3, P = g % 3, g % 2
        QT = QTp[P]
        for j in range(2):
            hj = h1t[s3][j]
            pm.rstd_from(tmB, hj, hj[:, :], 2, NORM_EPS, 1.0 / 32)
            yield
            op("vector", lambda e, hj=hj: e.tensor_scalar(out=xs[:, :], in0=hj[:, :], scalar1=st[:, 2:3], scalar2=None,
                                                         op0=ALU.mult), r=[hj, st], w=[xs])
            kb.transpose8(lambda bk, pv: op("vector", lambda e: e.tensor_tensor(
                out=xnT[:, :, :], in0=pv, in1=g1_fm[:, 0, :].unsqueeze(2).broadcast_to([128, 8, 128]), op=ALU.mult),
                r=[bk, g1_fm], w=[xnT]), xs)
            yield
            for which in range(2):
                for half in range(2):
                    bk = kb.bank()
                    c0 = which * 1024 + half * 512
                    cs = slice(half * 512, half * 512 + 512)
                    for ck in range(8):
                        op("tensor", lambda e, bk=bk, ck=ck, c0=c0: e.matmul(bk[:, :], lhsT=xnT[:, ck, :], rhs=Win[:, ck, c0:c0 + 512],
                                                                             start=(ck == 0), stop=(ck == 7)),
                           r=[xnT, Win], w=[bk] if ck == 0 else [], a=[bk] if ck else [])
                    if which == 0:
                        op("scalar", lambda e, bk=bk, cs=cs: e.mul(out=Qf[:, cs], in_=bk[:, :], mul=0.125),
                           r=[bk], w=[Qf] if half == 0 else [], a=[Qf] if half else [])
                    else:
                        sj = sgb[P][j]
                        op("scalar", lambda e, bk=bk, cs=cs, sj=sj: e.activation(out=sj[:, cs], in_=bk[:, :], func=AF.Silu),
                           r=[bk], w=[sj] if half == 0 else [], a=[sj] if half else [])
                    yield
            rope_ops(kb, Qf, Qb, cst[s3][j], rtmp, 0)
            yield

            def evq(bk, pv, j=j, QT=QT):
                op("scalar", lambda e: e.copy(out=QT[0:64, :, 0, 128 * j:128 * j + 128], in_=pv[0:64, :, :]),
                   r=[bk], w=[QT] if j == 0 else [], a=[QT] if j else [])
                op("scalar", lambda e: e.copy(out=QT[64:128, :, 1, 128 * j:128 * j + 128], in_=pv[64:128, :, :]),
                   r=[bk], a=[QT])
            kb.transpose8(evq, Qb)
            yield

    def attn_g(g):
        P = g % 2
        QT = QTp[P]
        n = 2 * g + 2
        steps = [(h, kt) for h in range(8) for kt in range(n)]
        banks = {}

        def emit_qk(h, kt):
            c0 = (kt // CH) * CH
            ci = cidx[(g, h, c0)]
            if kt == c0:
                issue_chunks(ci + NB - 1)
            KTs = KTc[ci % NB]
            kl = kt - c0
            bk = abanks[acnt[0] % 3]
            acnt[0] += 1
            banks[(h, kt)] = bk
            op("tensor", lambda e, bk=bk, kl=kl, h=h, KTs=KTs: e.matmul(
                bk[:, :], lhsT=KTs[:, kl * 128:(kl + 1) * 128],
                rhs=QT[:, h, :, :].rearrange("p c q -> p (c q)"), start=True, stop=True),
               r=[KTs, QT], w=[bk])

        def emit_rest(h, kt, sidx):
            c0 = (kt // CH) * CH
            ci = cidx[(g, h, c0)]
            Vs = Vc[ci % NB]
            kl = kt - c0
            lastk = (kt == n - 1)
            q0 = 128 if lastk else 0
            bk = banks.pop((h, kt))
            pt = PT[sidx % 3]
            bk3 = bk[:, :].rearrange("p (c q) -> p c q", q=256)
            op("scalar", lambda e, bk3=bk3, pt=pt, q0=q0: e.activation(out=pt[:, :, q0:256], in_=bk3[:, :, q0:256], func=AF.Exp),
               r=[bk], w=[pt])
            if kt >= n - 2:
                qd = 128 * (kt - (n - 2))
                op("gpsimd", lambda e, pt=pt, qd=qd: e.tensor_tensor(
                    out=pt[:, :, qd:qd + 128], in0=pt[:, :, qd:qd + 128],
                    in1=kb.mask_kq[:, :].unsqueeze(1).broadcast_to([128, 2, 128]), op=ALU.mult), r=[pt, kb.mask_kq], w=[pt])
            for qt in range(2):
                if lastk and qt == 0:
                    continue
                lastq = (kt == (n - 2 if qt == 0 else n - 1))
                for c in range(2):
                    ab = acc[qt][c]
                    op("tensor", lambda e, ab=ab, pt=pt, c=c, qt=qt, kl=kl, kt=kt, lastq=lastq, Vs=Vs: e.matmul(
                        ab[:, 0:129], lhsT=pt[:, c, 128 * qt:128 * qt + 128], rhs=Vs[:, kl, :], start=(kt == 0), stop=lastq),
                       r=[pt, Vs], w=[ab] if kt == 0 else [], a=[ab] if kt else [])

        def finalize(h):
            for qt in range(2):
                a0, a1 = acc[qt]
                rr_, t0_, ob_ = rr2[qt], t02[qt], ob2[qt]
                op("vector", lambda e, a0=a0, rr_=rr_: e.reciprocal(out=rr_[:, 0:1], in_=a0[:, 128:129]), r=[a0], w=[rr_])
                op("vector", lambda e, a1=a1, rr_=rr_: e.reciprocal(out=rr_[:, 1:2], in_=a1[:, 128:129]), r=[a1, rr_], w=[rr_])
                op("vector", lambda e, rr_=rr_: e.tensor_tensor(out=rr_[:, 2:3], in0=rr_[:, 1:2], in1=neglam, op=ALU.mult),
                   r=[rr_, lam], w=[rr_])
                op("vector", lambda e, a0=a0, rr_=rr_, t0_=t0_: e.tensor_scalar(out=t0_[:, :], in0=a0[:, 0:128], scalar1=rr_[:, 0:1],
                                                                             scalar2=None, op0=ALU.mult), r=[a0, rr_], w=[t0_])
                op("vector", lambda e, a1=a1, rr_=rr_, t0_=t0_, ob_=ob_: e.scalar_tensor_tensor(
                    out=ob_[:, :], in0=a1[:, 0:128], scalar=rr_[:, 2:3], in1=t0_[:, :], op0=ALU.mult, op1=ALU.add),
                   r=[a1, rr_, t0_], w=[ob_])
            for qt in range(2):
                rr_, t0_, ob_ = rr2[qt], t02[qt], ob2[qt]
                op("gpsimd", lambda e: e.memset(ast[:, 0:1], 0.0), w=[ast])
                op("scalar", lambda e, ob_=ob_: e.activation(out=ajunk[:, :], in_=ob_[:, :], func=AF.Square,
                                                             scale=1.0 / math.sqrt(128.0), accum_out=ast[:, 0:1]),
                   r=[ob_, ast], w=[ajunk, ast])
                op("scalar", lambda e: e.activation(out=ast[:, 0:1], in_=ast[:, 0:1], func=AF.Ln, bias=SUBLN_EPS, scale=1.0),
                   r=[ast], w=[ast])
                op("scalar", lambda e: e.activation(out=ast[:, 0:1], in_=ast[:, 0:1], func=AF.Exp, scale=-0.5),
                   r=[ast], w=[ast])
                op("vector", lambda e, ob_=ob_, t0_=t0_: e.scalar_tensor_tensor(out=t0_[:, :], in0=ob_[:, :], scalar=ast[:, 0:1],
                                                                               in1=bc_sub[:, :], op0=ALU.mult, op1=ALU.mult),
                   r=[ob_, ast, bc_sub], w=[t0_])
                ogq, sq_ = og[P][qt], sgb[P][qt]
                hs = slice(h * 128, (h + 1) * 128)
                op("gpsimd", lambda e, ogq=ogq, sq_=sq_, hs=hs, t0_=t0_: e.tensor_tensor(out=ogq[:, hs], in0=t0_[:, :], in1=sq_[:, hs],
                                                                                       op=ALU.mult),
                   r=[t0_, sq_], w=[ogq] if h == 0 else [], a=[ogq] if h else [])

        LA = 2
        for k_ in range(min(LA, len(steps))):
            emit_qk(*steps[k_])
        for i, (h, kt) in enumerate(steps):
            if i + LA < len(steps):
                emit_qk(*steps[i + LA])
            emit_rest(h, kt, i)
            if kt == n - 1:
                finalize(h)
                yield
            elif kt % 8 == 7:
                yield

    lasts = []

    def post_g(g):
        s3, P = g % 3, g % 2
        for j in range(2):
            it = 2 * g + j
            sl = slice(it * 128, (it + 1) * 128)
            yield from pm.run_g(og[P][j], h1t[s3][j], ptl[s3][j], hout, tmB)
            lasts.append(kb.dma("sync", out[sl, :], hout[:, :], r=[hout], key="stout"))
            yield

    def chain(*gs):
        for g_ in gs:
            yield from g_

    loads(0)
    issue_chunks(NB - 1)
    for _ in pre_g(0):
        pass
    for g in range(NG):
        side = []
        if g + 1 < NG:
            loads(g + 1)
        if g >= 1:
            side.append(post_g(g - 1))
        if g + 1 < NG:
            side.append(pre_g(g + 1))
        interleave([attn_g(g), chain(*side)])
    for _ in post_g(NG - 1):
        pass
    kb.mem.reset(m0)
    kb.rot = [0, 1, 2, 3, 4, 5, 6]
    return lasts[-1]


from concourse.bass_utils import run_bass_kernel_spmd

W_NAMES = ["norm_pre", "norm_post", "a_mu", "a_w_in", "a_w0", "a_w1", "a_w2", "a_a0", "a_a1", "a_a2", "a_k_k", "a_k_a",
           "a_r_k", "a_lnx_g", "a_lnx_b", "a_w_out", "kv_norm", "kv_w", "b_w_in", "b_lambda", "b_subln", "b_w_out",
           "ple_w", "ple_gate", "ple_norm"]
STRIP = {"a_mu", "a_w_in", "a_w0", "a_w1", "a_w2", "a_a0", "a_a1", "a_a2", "a_k_k", "a_k_a", "a_r_k", "a_lnx_g", "a_lnx_b",
         "a_w_out", "b_w_in", "b_lambda", "b_subln", "b_w_out"}


def prep_weights(inputs):
    w = {}
    for k in W_NAMES:
        v = np.ascontiguousarray(np.asarray(inputs[k], dtype=np.float32))
        if k in STRIP:
            v = v.reshape(v.shape[1:])
        if k == "a_r_k":
            v = v.reshape(-1)
        w[k] = v
    return w


def build_program(S, wshapes, cshapes):
    nc = bass.Bass("TRN2", target_bir_lowering=False)
    dram = {}
    dram["x"] = nc.dram_tensor("x", [S, D], F32, kind="ExternalInput").ap()
    dram["p"] = nc.dram_tensor("p", [2, S, PLE], F32, kind="ExternalInput").ap()
    for k, shp in wshapes.items():
        dram[k] = nc.dram_tensor(k, list(shp), F32, kind="ExternalInput").ap()
    for k, shp in cshapes.items():
        dram[k] = nc.dram_tensor(k, list(shp), F32, kind="ExternalInput").ap()
    out = nc.dram_tensor("out", [S, D], F32, kind="ExternalOutput").ap()
    yg = nc.dram_tensor("scr_yg", [S, D], BF16).ap()
    h1 = nc.dram_tensor("scr_h1", [S, D], F32).ap()
    kt = nc.dram_tensor("scr_kt", [8, 128, S], BF16).ap()
    vv = nc.dram_tensor("scr_v", [S, D], BF16).ap()
    kb = KB(nc, S)
    kb.load_consts(dram)
    phase_a1(kb, dram, yg)
    phase_a2(kb, dram, yg, h1, kt, vv)
    last = phase_b(kb, dram, h1, kt, vv, out)
    kb.c.emit(final_wait_ops=[last])
    return nc, kb


def make_const_inputs(S):
    c = {"c_" + k: v for k, v in make_consts().items()}
    cos, sin = rope_tables(S)
    c["c_cos"] = cos
    c["c_sin"] = sin
    return c


_CACHE = {}


def kernel(**inputs):
    x = np.asarray(inputs["x"], dtype=np.float32)
    p = np.asarray(inputs["p"], dtype=np.float32)
    B, S, _ = x.shape
    w = prep_weights(inputs)
    consts = make_const_inputs(S)
    key = (S,)
    if key not in _CACHE:
        _CACHE[key] = build_program(S, {k: v.shape for k, v in w.items()}, {k: v.shape for k, v in consts.items()})
    nc, kb = _CACHE[key]
    in_maps = []
    for b in range(B):
        m = {"x": np.ascontiguousarray(x[b]), "p": np.ascontiguousarray(p[:, b])}
        m.update(w)
        m.update(consts)
        in_maps.append(m)
    res = run_bass_kernel_spmd(nc, in_maps, core_ids=list(range(B)))
    return np.stack([np.asarray(r["out"], dtype=np.float32) for r in res.results], 0)
```

```python
import numpy as np
import concourse.bass as bass
import concourse.mybir as mybir

F32 = mybir.dt.float32
BF16 = mybir.dt.bfloat16
AF = mybir.ActivationFunctionType
ALU = mybir.AluOpType
AX = mybir.AxisListType

ENGS = ("tensor", "vector", "scalar", "gpsimd", "sync")
EPOCH = 12000


class Buf:
    __slots__ = ("name", "w", "r", "prev_r", "excl")

    def __init__(self, name):
        self.name = name
        self.excl = False
        self.w = []
        self.r = []
        self.prev_r = []


class Op:
    __slots__ = ("eng", "fn", "deps", "dma_key", "signal", "tok", "idx")

    def __init__(self, eng, fn, dma_key=None):
        self.eng = eng
        self.fn = fn
        self.deps = []
        self.dma_key = dma_key
        self.signal = dma_key is not None
        self.tok = None


class PEProxy:
    def __init__(self, eng, fence):
        self.eng = eng
        self.fence = fence
        self.prev = None
        self.nfence = 0

    def _chk(self, ap):
        key = (ap.base_partition(), ap.shape[0])
        partial = key[1] < 128
        if partial and self.prev is not None and self.prev != key and self.fence is not None:
            self.fence(self.eng)
            self.nfence += 1
        self.prev = key if partial else None

    def matmul(self, out, lhsT, rhs, **kw):
        self._chk(lhsT)
        return self.eng.matmul(out, lhsT=lhsT, rhs=rhs, **kw)

    def transpose(self, out, in_, identity, **kw):
        self._chk(in_)
        return self.eng.transpose(out=out, in_=in_, identity=identity, **kw)

    def wait_ge(self, *a, **k):
        return self.eng.wait_ge(*a, **k)


class Ctx:
    pe_fence = None

    def __init__(self, nc, same_engine_sync=True):
        self.nc = nc
        self.ops = {e: [] for e in ENGS}
        self.same_engine_sync = same_engine_sync
        self.nbuf = 0

    def buf(self, name=None):
        self.nbuf += 1
        return Buf(name or f"b{self.nbuf}")

    def op(self, eng, fn, reads=(), writes=(), appends=(), dma_key=None):
        o = Op(eng, fn, dma_key)
        deps = o.deps
        for b in reads:
            deps.extend(b.w)
            if b.excl:
                deps.extend(x for x in b.r if x.eng != eng)
        for b in writes:
            deps.extend(b.w)
            deps.extend(b.r)
        for b in appends:
            deps.extend(b.prev_r)
            deps.extend(b.w[:1])
        for b in reads:
            b.r.append(o)
        for b in writes:
            b.prev_r = b.r
            b.w = [o]
            b.r = []
        for b in appends:
            b.w.append(o)
        self.ops[eng].append(o)
        return o

    def dma(self, eng, out, in_, reads=(), writes=(), appends=(), key=None, slow=False):
        if slow:
            fn = lambda e: e.dma_start(out=out, in_=in_, allow_slow_non_contiguous=True)
        else:
            fn = lambda e: e.dma_start(out=out, in_=in_)
        return self.op(eng, fn, reads, writes, appends, dma_key=key or "dma_" + eng)

    def barrier(self):
        lasts = []
        for e in ENGS:
            ops = self.ops[e]
            for o in reversed(ops):
                if o.dma_key is None and o.fn is not None:
                    lasts.append(o)
                    break
            seen = set()
            for o in reversed(ops):
                if o.dma_key is not None and o.dma_key not in seen:
                    seen.add(o.dma_key)
                    lasts.append(o)
        for e in ENGS:
            o = Op(e, None)
            o.deps = list(lasts)
            self.ops[e].append(o)

    def emit(self, final_wait_ops=()):
        nc = self.nc
        for e in ENGS:
            for o in self.ops[e]:
                for d in o.deps:
                    if d.eng == e and d.dma_key is None:
                        if e == "tensor" or not self.same_engine_sync:
                            continue
                    d.signal = True
        for o in final_wait_ops:
            o.signal = True
        sems = {}
        import contextlib
        stack = contextlib.ExitStack()

        def getsem(key):
            if key not in sems:
                sems[key] = stack.enter_context(nc.semaphore(key))
            return sems[key]

        dma_cnt = {}
        for e in ENGS:
            cnt = 0
            for o in self.ops[e]:
                if o.fn is None:
                    continue
                if o.dma_key is not None:
                    dma_cnt[o.dma_key] = dma_cnt.get(o.dma_key, 0) + 16
                    c = dma_cnt[o.dma_key]
                    ep, v = divmod(c - 16, EPOCH * 16)
                    o.tok = (f"{o.dma_key}_{ep}", v + 16)
                elif o.signal:
                    ep, v = divmod(cnt, EPOCH)
                    o.tok = (f"s_{e}_{ep}", v + 1)
                    cnt += 1
        for e in ENGS:
            for o in self.ops[e]:
                if o.tok is not None:
                    getsem(o.tok[0])
        self.nsem = len(sems)
        with stack, nc.Block() as block:
            def run(e, engobj, extra_final=False):
                seen = {}
                nwait = 0
                for o in self.ops[e]:
                    need = {}
                    for d in o.deps:
                        if d.tok is None:
                            continue
                        if d.eng == e and d.dma_key is None and (
                                e == "tensor" or not self.same_engine_sync):
                            continue
                        k, v = d.tok
                        if seen.get(k, 0) < v and need.get(k, 0) < v:
                            need[k] = v
                    for k, v in need.items():
                        engobj.wait_ge(sems[k], v)
                        seen[k] = v
                        nwait += 1
                    if o.fn is None:
                        continue
                    ins = o.fn(engobj)
                    if o.tok is not None:
                        ins.then_inc(sems[o.tok[0]], 16 if o.dma_key is not None else 1)
                if extra_final:
                    for o in final_wait_ops:
                        k, v = o.tok
                        if seen.get(k, 0) < v:
                            engobj.wait_ge(sems[k], v)
                            seen[k] = v

            @block.tensor
            def _(t):
                px = PEProxy(t, self.pe_fence)
                run("tensor", px)
                self.nfence = px.nfence

            @block.vector
            def _(v):
                run("vector", v)

            @block.scalar
            def _(s):
                run("scalar", s)

            @block.gpsimd
            def _(g):
                run("gpsimd", g)

            @block.sync
            def _(s):
                run("sync", s, extra_final=True)


import math
import numpy as np
import ml_dtypes

D = 1024
PLE = 256
NH = 16
HD = 64
C = 64
DECAY_C = -math.exp(-0.5)
GN_EPS = 64e-5
NORM_EPS = 1e-6
SUBLN_EPS = 1e-5


class Mem:
    def __init__(self, nc, ctx, limit=229376):
        self.nc = nc
        self.ctx = ctx
        self.off = 16640
        self.limit = limit
        self.n = 0
        self.peak = 0

    def alloc(self, name, shape, dtype):
        sz = 4 if dtype == F32 else 2
        nbytes = int(np.prod(shape[1:])) * sz
        nbytes = (nbytes + 63) // 64 * 64
        self.n += 1
        t = self.nc.alloc_sbuf_tensor_at(f"{name}_{self.n}", list(shape), dtype, offset=self.off)
        self.off += nbytes
        self.peak = max(self.peak, self.off)
        assert self.off <= self.limit, f"SBUF overflow at {name}: {self.off}"
        return t, self.ctx.buf(name)

    def mark(self):
        return self.off

    def reset(self, off):
        self.off = off


class T:
    def __init__(self, h, buf):
        self.h = h
        self.buf = buf

    def __getitem__(self, k):
        return self.h[k]


def make_consts():
    c = {}
    c["ident"] = np.eye(128, dtype=np.float32)
    s = np.arange(128)[:, None]
    t = np.arange(128)[None, :]
    same = (s // C) == (t // C)
    incl = (same & (s <= t)).astype(np.float32)
    strict = (same & (s < t)).astype(np.float32)
    suffix = (same & (s > t)).astype(np.float32)
    c["tri"] = np.stack([incl * DECAY_C, strict * DECAY_C, suffix * DECAY_C], 1).reshape(128, 384).astype(np.float32)
    sel = np.zeros((128, 2), np.float32)
    sel[:64, 0] = DECAY_C
    sel[64:, 1] = DECAY_C
    c["sel"] = sel
    m_si = np.concatenate([strict, incl], 1)
    m_tt = np.concatenate([strict.T, strict.T], 1)
    c["mask_si"] = np.concatenate([m_si, m_si], 1).astype(np.float32)
    c["mask_tt"] = np.concatenate([m_tt, m_tt], 1).astype(np.float32)
    c["mask_kq"] = (t >= s).astype(np.float32)
    return c


def rope_tables(S):
    inv = 500000.0 ** (-np.arange(0, 16, 2, dtype=np.float32) / 16)
    ang = np.arange(S, dtype=np.float32)[:, None] * inv[None, :]
    return np.cos(ang).astype(np.float32), np.sin(ang).astype(np.float32)


class KB:
    def __init__(self, nc, S):
        self.nc = nc
        self.S = S
        self.c = Ctx(nc)
        self.mem = Mem(nc, self.c)
        self.ps = []
        for i in range(8):
            h = nc.alloc_psum_tensor(f"psb{i}", [128, 512], F32)
            self.ps.append(T(h, self.c.buf(f"ps{i}")))
            self.ps[-1].buf.excl = True
        self.rr = 0
        self.rot = [0, 1, 2, 3, 4, 5, 6]
        self.cv = 0

    def bank(self):
        b = self.ps[self.rot[self.rr % len(self.rot)]]
        self.rr += 1
        return b

    def op(self, eng, fn, r=(), w=(), a=()):
        return self.c.op(eng, fn, [t.buf for t in r], [t.buf for t in w], [t.buf for t in a])

    def dma(self, eng, out, in_, r=(), w=(), a=(), key=None, slow=False):
        return self.c.dma(eng, out, in_, [t.buf for t in r], [t.buf for t in w],
                          [t.buf for t in a], key=key, slow=slow)

    def alloc(self, name, shape, dtype):
        h, b = self.mem.alloc(name, shape, dtype)
        return T(h, b)

    def cvt_eng(self):
        self.cv += 1
        return ("gpsimd", "vector")[self.cv % 2]

    def load_consts(self, dram):
        al = self.alloc
        self.ident_f = al("ident_f", [128, 128], F32)
        self.ident = al("ident", [128, 128], BF16)
        self.tri = al("tri", [128, 384], F32)
        self.sel = al("sel", [128, 2], F32)
        self.mask_si = al("mask_si", [128, 512], BF16)
        self.mask_tt = al("mask_tt", [128, 512], BF16)
        self.mask_kq = al("mask_kq", [128, 128], BF16)
        self.ones_f = al("ones_f", [1, 128], F32)
        stg = al("cstg", [128, 512], F32)
        self.dma("sync", self.ident_f[:, :], dram["c_ident"], w=[self.ident_f], key="c0")
        self.op("vector", lambda e: e.tensor_copy(out=self.ident[:, :], in_=self.ident_f[:, :]),
                r=[self.ident_f], w=[self.ident])
        self.dma("sync", self.tri[:, :], dram["c_tri"], w=[self.tri], key="c1")
        self.dma("sync", self.sel[:, :], dram["c_sel"], w=[self.sel], key="c2")
        for nm, dst, n in (("c_mask_si", self.mask_si, 512), ("c_mask_tt", self.mask_tt, 512),
                           ("c_mask_kq", self.mask_kq, 128)):
            self.dma("sync", stg[:, 0:n], dram[nm], w=[stg], key="c3")
            self.op("vector", lambda e, dst=dst, n=n: e.tensor_copy(out=dst[:, :], in_=stg[:, 0:n]),
                    r=[stg], w=[dst])
        self.op("vector", lambda e: e.memset(self.ones_f[:, :], 1.0), w=[self.ones_f])
        fz = al("fz", [128, 32], BF16)
        self.op("vector", lambda e: e.memset(fz[:, :], 0.0), w=[fz])
        scr = self.ps[7]

        def fence(e):
            return e.matmul(scr[0:32, 0:8], lhsT=fz[:, 0:32], rhs=fz[:, 0:8], start=True, stop=True)
        self.op("tensor", fence, r=[fz])
        self.c.pe_fence = fence

    def load_w_bf16(self, dst, src, ncols, stg, col0=0, nck=8):
        srcv = src.rearrange("(c p) n -> p c n", p=128)
        for ck in range(nck):
            for n0 in range(0, ncols, 1024):
                n1 = min(ncols, n0 + 1024)
                st = stg[self.cv % len(stg)]
                self.dma("sync", st[:, 0:n1 - n0], srcv[:, ck, n0:n1], w=[st], key=f"wst{self.cv % len(stg)}")
                eng = self.cvt_eng()
                self.op(eng, lambda e, st=st, ck=ck, n0=n0, n1=n1: e.tensor_copy(
                    out=dst[:, ck, col0 + n0:col0 + n1], in_=st[:, 0:n1 - n0]), r=[st], a=[dst])

    def load_bcast(self, dst, src_vec, stg=None, key="bc"):
        n = src_vec.shape[0]
        if stg is None:
            self.dma("sync", dst[:, :], src_vec.partition_broadcast(128), w=[dst], key=key)
        else:
            self.dma("sync", stg[:, 0:n], src_vec.partition_broadcast(128), w=[stg], key=key)
            self.op("vector", lambda e: e.tensor_copy(out=dst[:, :], in_=stg[:, 0:n]), r=[stg], w=[dst])

    def load_fm_vecs(self, dst, vecs, stg):
        nv = len(vecs)
        rows = stg
        for i, v in enumerate(vecs):
            self.dma("sync", rows[8 * i:8 * i + 8, 0:128], v.rearrange("(c p) -> c p", p=128),
                     w=[rows] if i == 0 else [], a=[rows] if i else [], key="fmv")
        bk = self.bank()
        n = nv * 8
        self.op("tensor", lambda e: e.transpose(out=bk[:, 0:n], in_=rows[0:n, 0:128], identity=self.ident_f[0:n, 0:n]),
                r=[rows, self.ident_f], w=[bk])
        self.op("vector", lambda e: e.tensor_copy(out=dst[:, :, :].rearrange("p v c -> p (v c)"), in_=bk[:, 0:n]),
                r=[bk], w=[dst])

    def transpose8(self, dst_fn, src, nck=8, evac=None):
        bk = self.bank()
        pv = bk[:, :].bitcast(BF16).rearrange("p (c t) -> p c t", t=128)
        for ck in range(nck):
            self.op("tensor", lambda e, ck=ck: e.transpose(out=pv[:, ck, :], in_=src[:, ck * 128:(ck + 1) * 128],
                                                           identity=self.ident[:, :]),
                    r=[src, self.ident], w=[bk] if ck == 0 else [], a=[bk] if ck else [])
        dst_fn(bk, pv)


def phase_a1(kb, dram, yg_out, stop=None, dbg=None):
    nc, S = kb.nc, kb.S
    al = kb.alloc
    op = kb.op
    NT = S // 128
    m0 = kb.mem.mark()
    Win = al("Win", [128, 8, 4096], BF16)
    W1A1 = al("W1A1", [128, 8, 128], BF16)
    W2A2 = al("W2A2", [128, 1024], BF16)
    w0hl = al("w0hl", [33, 2048], BF16)
    ones_b = al("ones_b", [33, 128], BF16)
    mu_fm = al("mu_fm", [128, 7, 8], F32)
    bc_kk = al("bc_kk", [128, 1024], BF16)
    bc_ka = al("bc_ka", [128, 1024], BF16)
    bc_rk = al("bc_rk", [128, 1024], BF16)
    bc_lg = al("bc_lg", [128, 1024], BF16)
    bc_lb = al("bc_lb", [128, 1024], BF16)
    m1 = kb.mem.mark()
    stg = [al(f"wstg{i}", [128, 1024], F32) for i in range(3)]
    for ci in range(4):
        kb.load_w_bf16(Win, dram["a_w_in"][ci], 1024, stg, col0=ci * 1024)
    kb.load_w_bf16(W1A1, dram["a_w1"], 64, stg, col0=0)
    kb.load_w_bf16(W1A1, dram["a_a1"], 64, stg, col0=64)
    st = stg[0]
    kb.dma("sync", st[0:64, 0:1024], dram["a_w2"], w=[st], key="wst0")
    kb.dma("sync", st[64:128, 0:1024], dram["a_a2"], a=[st], key="wst0")
    op("vector", lambda e: e.tensor_copy(out=W2A2[:, :], in_=st[:, :]), r=[st], w=[W2A2])
    op("vector", lambda e: e.memset(w0hl[:, :], 0.0), w=[w0hl])
    op("vector", lambda e: e.memset(ones_b[:, :], 1.0), w=[ones_b])
    sA, sB = stg[1], stg[2]
    for pr in (0, 32):
        kb.dma("sync", sA[pr:pr + 1, 0:1024], dram["a_w0"].unsqueeze(0), w=[sA] if pr == 0 else [], a=[sA] if pr else [], key="sm0")
        kb.dma("sync", sB[pr:pr + 1, 0:1024], dram["a_a0"].unsqueeze(0), w=[sB] if pr == 0 else [], a=[sB] if pr else [], key="sm3")
    for src_, c0 in ((sA, 0), (sB, 1024)):
        op("vector", lambda e, src_=src_, c0=c0: e.tensor_copy(out=w0hl[0:1, c0:c0 + 1024], in_=src_[0:1, 0:1024]),
           r=[src_], a=[w0hl])
        op("vector", lambda e, src_=src_, c0=c0: e.tensor_copy(out=w0hl[32:33, c0:c0 + 1024], in_=src_[32:33, 0:1024]),
           r=[src_], a=[w0hl])
        op("vector", lambda e, src_=src_, c0=c0: e.tensor_tensor(out=src_[32:33, 0:1024], in0=src_[32:33, 0:1024],
                                                                in1=w0hl[32:33, c0:c0 + 1024], op=ALU.subtract),
           r=[w0hl, src_], w=[src_])
        op("vector", lambda e, src_=src_, c0=c0: e.tensor_copy(out=w0hl[32:33, c0:c0 + 1024], in_=src_[32:33, 0:1024]),
           r=[src_], w=[w0hl])
    kb.load_fm_vecs(mu_fm, [dram["a_mu"][ci] for ci in range(6)] + [dram["norm_pre"][0]], stg[2])
    for dst, nm in ((bc_kk, "a_k_k"), (bc_ka, "a_k_a"), (bc_rk, "a_r_k"), (bc_lg, "a_lnx_g"), (bc_lb, "a_lnx_b")):
        kb.load_bcast(dst, dram[nm], stg=stg[1])
    kb.c.barrier()
    kb.mem.reset(m1)
    class Stop(Exception):
        pass
    def dump(k, t, f32=False):
        if stop == k:
            v = t[:, :] if len(t.h.shape) == 2 else t[:, :, :].rearrange("p a b -> p (a b)")
            n = v.shape[1]
            kb.last = kb.dma("sync", (dbg if f32 else yg_out)[0:128, 0:n], v, r=[t], key="dbg")
            raise Stop()
    dump(0, W2A2)
    xt = [al("xt0", [128, 1024], F32)]
    xs = al("xs", [128, 1024], BF16)
    xnT = [al(f"xnT{i}", [128, 8, 129], BF16) for i in range(2)]
    dxT = al("dxT", [128, 8, 128], BF16)
    tmpm = al("tmpm", [128, 8, 128], BF16)
    xm = [al(f"xm{i}", [128, 8, 128], BF16) for i in range(2)]
    F = [al(f"F{i}", [128, 1024], F32) for i in range(5)]
    Vb = al("Vb", [128, 1024], BF16)
    sgb = al("sgb", [128, 1024], BF16)
    E = [al(f"E{i}", [128, 1024], BF16) for i in range(4)]
    O = [al(f"O{i}", [128, 1024], BF16) for i in range(6)]
    hT = al("hT", [128, 128], BF16)
    ARTz = [al(f"ARTz{i}", [128, 8, 2, 128], BF16) for i in range(2)]
    BKT = al("BKT", [128, 8, 2, 128], BF16)
    gC = [al(f"gC{i}", [128, 8, 2], F32) for i in range(2)]
    sm = [al(f"sm{i}", [128, 16], F32) for i in range(6)]
    ss = al("ss", [128, 2], F32)
    MIs = [al(f"MI{s_}", [128, 4, 2, 128], BF16) for s_ in range(2)]
    MKs = [al(f"MK{s_}", [128, 4, 2, 128], BF16) for s_ in range(2)]
    MTs = [al(f"MT{s_}", [128, 4, 2, 128], BF16) for s_ in range(2)]
    Nks = [[al(f"Nk{s_}_{i}", [128, 4, 128], BF16) for i in range(2)] for s_ in range(2)]
    NkTs = [[al(f"NkT{s_}_{i}", [128, 4, 128], BF16) for i in range(2)] for s_ in range(2)]
    Tks = [[al(f"Tk{s_}_{i}", [128, 4, 128], BF16) for i in range(2)] for s_ in range(2)]
    Gs = [al(f"G{s_}", [128, 4, 128], BF16) for s_ in range(2)]
    Stmps = [al(f"Stmp{s_}", [128, 2, 64], F32) for s_ in range(2)]
    WTh = [al(f"WT{g_}", [128, 256], BF16) for g_ in range(4)]
    Abh = [al(f"Ab{g_}", [128, 256], BF16) for g_ in range(4)]
    RbTh = [al(f"RbT{g_}", [128, 2, 128], BF16) for g_ in range(4)]
    Pph = [al(f"Pp{g_}", [128, 2, 2, 64], F32) for g_ in range(4)]
    Sth = [al(f"St{g_}", [128, 2, 64], F32) for g_ in range(4)]
    Sbh = [al(f"Sb{g_}", [128, 2, 64], BF16) for g_ in range(4)]
    ygt = al("ygt", [128, 1024], BF16)
    print("A1 sbuf peak", kb.mem.peak)
    ident = kb.ident

    for z_ in ARTz:
        op("gpsimd", lambda e, z_=z_: e.memset(z_[:, :, :, :], 0.0), w=[z_])
    for g_ in range(4):
        op("vector", lambda e, g_=g_: e.memset(Sth[g_][:, :, :], 0.0), w=[Sth[g_]])
        op("vector", lambda e, g_=g_: e.memset(Sbh[g_][:, :, :], 0.0), w=[Sbh[g_]])
    kb.rot = [0, 1, 2, 3]
    kb.rr = 0
    op("gpsimd", lambda e: e.memset(xnT[1][:, :, :], 0.0), w=[xnT[1]])
    op("gpsimd", lambda e: e.memset(xnT[0][:, :, :], 0.0), w=[xnT[0]])

    xv = dram["x"]
    kb.dma("sync", xt[0][:, :], xv[0:128, :], w=[xt[0]], key="ldx0")
    def mix(ci, dst, xc):
        op("vector", lambda e: e.tensor_tensor(out=tmpm[:, :, :], in0=dxT[:, :, :],
                                               in1=mu_fm[:, ci, :].unsqueeze(2).broadcast_to([128, 8, 128]),
                                               op=ALU.mult), r=[dxT, mu_fm], w=[tmpm])
        op("vector", lambda e: e.tensor_tensor(out=dst[:, :, :], in0=tmpm[:, :, :], in1=xc[:, :, 1:129],
                                               op=ALU.add), r=[tmpm, xc], w=[dst])

    def part1_g(it):
        cur = xt[0]
        xc, xp = xnT[it % 2], xnT[(it + 1) % 2]
        gCc = gC[it % 2]
        op("gpsimd", lambda e: e.memset(ss[:, :], 0.0), w=[ss])
        op("scalar", lambda e: e.activation(out=xs[:, :], in_=cur[:, :], func=AF.Square, scale=1.0 / 32,
                                            accum_out=ss[:, 0:1]), r=[cur], w=[xs, ss])
        op("scalar", lambda e: e.activation(out=ss[:, 1:2], in_=ss[:, 0:1], func=AF.Sqrt, bias=NORM_EPS, scale=1.0),
           r=[ss], w=[ss])
        op("vector", lambda e: e.reciprocal(out=ss[:, 1:2], in_=ss[:, 1:2]), r=[ss], w=[ss])
        op("vector", lambda e: e.tensor_scalar(out=xs[:, :], in0=cur[:, :], scalar1=ss[:, 1:2], scalar2=None,
                                               op0=ALU.mult), r=[cur, ss], w=[xs])
        if it + 1 < NT:
            kb.dma("sync", cur[:, :], xv[(it + 1) * 128:(it + 2) * 128, :], w=[cur], key="ldx0")
        yield

        def ev_xn(bk, pv, xc=xc, xp=xp):
            op("vector", lambda e: e.tensor_tensor(out=xc[:, :, 1:129], in0=pv,
                                                   in1=mu_fm[:, 6, :].unsqueeze(2).broadcast_to([128, 8, 128]),
                                                   op=ALU.mult), r=[bk, mu_fm], w=[xc])
            op("gpsimd", lambda e: e.tensor_copy(out=xc[:, :, 0:1], in_=xp[:, :, 128:129]), r=[xp], a=[xc])
        kb.transpose8(ev_xn, xs)
        yield
        op("vector", lambda e: e.tensor_tensor(out=dxT[:, :, :], in0=xc[:, :, 0:128], in1=xc[:, :, 1:129],
                                               op=ALU.subtract), r=[xc], w=[dxT])
        yield
        bkh = kb.ps[4]
        for li, ci in enumerate((4, 5)):
            xmd = xm[li]
            mix(ci, xmd, xc)
            for ck in range(8):
                first = (ck == 0)
                op("tensor", lambda e, ck=ck, li=li, xmd=xmd, first=first: e.matmul(
                    bkh[64 * li:64 * li + 64, 0:128], lhsT=W1A1[:, ck, 64 * li:64 * li + 64], rhs=xmd[:, ck, :],
                    start=first, stop=(ck == 7)), r=[W1A1, xmd],
                   w=[bkh] if (first and li == 0) else [], a=[] if (first and li == 0) else [bkh])
            yield
        op("scalar", lambda e: e.activation(out=hT[0:64, :], in_=bkh[0:64, 0:128], func=AF.Tanh), r=[bkh], w=[hT])
        op("scalar", lambda e: e.copy(out=hT[64:128, :], in_=bkh[64:128, 0:128]), r=[bkh], a=[hT])
        yield
        for li, dst in ((0, F[2]), (1, F[4])):
            for half in range(2):
                bk = kb.bank()
                cs = slice(half * 512, half * 512 + 512)
                op("tensor", lambda e, bk=bk, li=li, cs=cs: e.matmul(
                    bk[:, :], lhsT=hT[64 * li:64 * li + 64, :], rhs=W2A2[64 * li:64 * li + 64, cs],
                    start=True, stop=False), r=[hT, W2A2], w=[bk])
                op("tensor", lambda e, bk=bk, li=li, half=half: e.matmul(
                    bk[:, :], lhsT=ones_b[0:33, :], rhs=w0hl[0:33, li * 1024 + half * 512: li * 1024 + half * 512 + 512],
                    start=False, stop=True), r=[ones_b, w0hl], a=[bk])
                op("scalar", lambda e, bk=bk, dst=dst, cs=cs: e.activation(out=dst[:, cs], in_=bk[:, :], func=AF.Sigmoid),
                   r=[bk], w=[dst] if half == 0 else [], a=[dst] if half else [])
                yield
        sgw = F[2]
        expcfg = ((0, E[0], 1.0), (0, E[1], -1.0), (1, E[2], 1.0), (2, E[3], 1.0))
        for half in range(2):
            cs = slice(half * 512, half * 512 + 512)
            for ti in range(3):
                bk = kb.bank()
                op("tensor", lambda e, bk=bk, ti=ti, cs=cs: e.matmul(
                    bk[:, :], lhsT=kb.tri[:, ti * 128:(ti + 1) * 128], rhs=sgw[:, cs], start=True, stop=True),
                   r=[kb.tri, sgw], w=[bk])
                for (tj, dst, sc) in expcfg:
                    if tj != ti:
                        continue
                    op("scalar", lambda e, bk=bk, dst=dst, sc=sc, cs=cs: e.activation(
                        out=dst[:, cs], in_=bk[:, :], func=AF.Exp, scale=sc), r=[bk],
                       w=[dst] if half == 0 else [], a=[dst] if half else [])
                yield
        bk = kb.bank()
        for ck in range(8):
            op("tensor", lambda e, ck=ck, bk=bk: e.matmul(bk[:, 2 * ck:2 * ck + 2], lhsT=sgw[:, ck * 128:(ck + 1) * 128],
                                                          rhs=kb.sel[:, :], start=True, stop=True),
               r=[sgw, kb.sel], w=[bk] if ck == 0 else [], a=[bk] if ck else [])
        op("scalar", lambda e, bk=bk: e.activation(out=gCc[:, :, :].rearrange("p c t -> p (c t)"), in_=bk[:, 0:16],
                                                   func=AF.Exp), r=[bk], w=[gCc])
        yield
    def tile(it):
        xc = xnT[it % 2]
        gCc = gC[it % 2]
        if it == 0:
            for _ in part1_g(0):
                pass
        sgw, asig = F[2], F[4]
        rT, kT = F[0], F[1]
        for pi, ci in enumerate((0, 1, 2, 3)):
            xmd = xm[pi % 2]
            mix(ci, xmd, xc)
            for half in range(2):
                bk = kb.bank()
                cs = slice(half * 512, half * 512 + 512)
                for ck in range(8):
                    op("tensor", lambda e, bk=bk, ck=ck, xmd=xmd, ci=ci, half=half: e.matmul(
                        bk[:, :], lhsT=xmd[:, ck, :], rhs=Win[:, ck, ci * 1024 + half * 512: ci * 1024 + half * 512 + 512],
                        start=(ck == 0), stop=(ck == 7)), r=[xmd, Win], w=[bk] if ck == 0 else [], a=[bk] if ck else [])
                if ci == 0:
                    op("scalar", lambda e, bk=bk, cs=cs: e.copy(out=rT[:, cs], in_=bk[:, :]), r=[bk],
                       w=[rT] if half == 0 else [], a=[rT] if half else [])
                elif ci == 1:
                    op("vector", lambda e, bk=bk, cs=cs: e.tensor_copy(out=kT[:, cs], in_=bk[:, :]), r=[bk],
                       w=[kT] if half == 0 else [], a=[kT] if half else [])
                elif ci == 2:
                    op("scalar", lambda e, bk=bk, cs=cs: e.copy(out=Vb[:, cs], in_=bk[:, :]), r=[bk],
                       w=[Vb] if half == 0 else [], a=[Vb] if half else [])
                else:
                    op("scalar", lambda e, bk=bk, cs=cs: e.activation(out=sgb[:, cs], in_=bk[:, :], func=AF.Silu),
                       r=[bk], w=[sgb] if half == 0 else [], a=[sgb] if half else [])
        dump(6, sgb)
        eL, enL, eLx, eD = E
        h3 = lambda t: t[:, :].rearrange("p (h j) -> p h j", j=64)
        bc3 = lambda t: t[:, :].unsqueeze(2).broadcast_to([128, 16, 64])
        kk = F[2]
        op("vector", lambda e: e.tensor_tensor(out=kk[:, :], in0=kT[:, :], in1=bc_kk[:, :], op=ALU.mult),
           r=[kT, bc_kk], w=[kk])
        op("vector", lambda e: e.tensor_tensor(out=F[3][:, :], in0=kk[:, :], in1=kk[:, :], op=ALU.mult),
           r=[kk], w=[F[3]])
        op("vector", lambda e: e.tensor_reduce(out=sm[0][:, :], in_=h3(F[3]), axis=AX.X, op=ALU.add),
           r=[F[3]], w=[sm[0]])
        op("scalar", lambda e: e.activation(out=sm[0][:, :], in_=sm[0][:, :], func=AF.Sqrt), r=[sm[0]], w=[sm[0]])
        op("vector", lambda e: e.tensor_scalar_max(out=sm[0][:, :], in0=sm[0][:, :], scalar1=1e-12), r=[sm[0]], w=[sm[0]])
        op("vector", lambda e: e.reciprocal(out=sm[0][:, :], in_=sm[0][:, :]), r=[sm[0]], w=[sm[0]])
        op("vector", lambda e: e.tensor_tensor(out=h3(kk), in0=h3(kk), in1=bc3(sm[0]), op=ALU.mult),
           r=[kk, sm[0]], w=[kk])
        bT = F[3]
        op("gpsimd", lambda e: e.tensor_tensor(out=bT[:, :], in0=kk[:, :], in1=asig[:, :], op=ALU.mult),
           r=[kk, asig], w=[bT])
        kmod = F[4]
        op("vector", lambda e: e.scalar_tensor_tensor(out=kmod[:, :], in0=asig[:, :], scalar=-1.0, in1=bc_ka[:, :],
                                                      op0=ALU.add, op1=ALU.mult), r=[asig, bc_ka], w=[kmod])
        op("vector", lambda e: e.scalar_tensor_tensor(out=kmod[:, :], in0=kmod[:, :], scalar=1.0, in1=kT[:, :],
                                                      op0=ALU.add, op1=ALU.mult), r=[kmod, kT], w=[kmod])
        rk = F[1]
        op("vector", lambda e: e.tensor_tensor(out=rk[:, :], in0=rT[:, :], in1=kmod[:, :], op=ALU.mult),
           r=[rT, kmod], w=[rk])
        op("vector", lambda e: e.tensor_tensor(out=rk[:, :], in0=rk[:, :], in1=bc_rk[:, :], op=ALU.mult),
           r=[rk, bc_rk], w=[rk])
        bsum = sm[1]
        op("vector", lambda e: e.tensor_reduce(out=bsum[:, :], in_=h3(rk), axis=AX.X, op=ALU.add), r=[rk], w=[bsum])
        Rt, Kt, Bt, At, Kg, Bg = O
        op("vector", lambda e: e.tensor_tensor(out=Rt[:, :], in0=rT[:, :], in1=eL[:, :], op=ALU.mult), r=[rT, eL], w=[Rt])
        op("gpsimd", lambda e: e.tensor_tensor(out=Kt[:, :], in0=kmod[:, :], in1=enL[:, :], op=ALU.mult), r=[kmod, enL], w=[Kt])
        op("vector", lambda e: e.tensor_tensor(out=Bt[:, :], in0=bT[:, :], in1=enL[:, :], op=ALU.mult), r=[bT, enL], w=[Bt])
        op("vector", lambda e: e.scalar_tensor_tensor(out=At[:, :], in0=kk[:, :], scalar=-1.0, in1=eLx[:, :],
                                                      op0=ALU.mult, op1=ALU.mult), r=[kk, eLx], w=[At])
        op("vector", lambda e: e.tensor_tensor(out=Kg[:, :], in0=kmod[:, :], in1=eD[:, :], op=ALU.mult), r=[kmod, eD], w=[Kg])
        op("gpsimd", lambda e: e.tensor_tensor(out=Bg[:, :], in0=bT[:, :], in1=eD[:, :], op=ALU.mult), r=[bT, eD], w=[Bg])
        dump(7, Bg)
        for src, dst, slot, eng in ((At, None, 0, "scalar"), (Rt, None, 1, "vector"), (Bt, BKT, 0, "scalar"), (Kt, BKT, 1, "vector")):
            def ev(bk, pv, dst=dst, slot=slot, eng=eng):
                if dst is None:
                    for par in range(2):
                        z_ = ARTz[par]
                        pr = slice(64 * par, 64 * par + 64)
                        ww = [z_] if slot == 0 else []
                        aa = [z_] if slot else []
                        if eng == "scalar":
                            op("scalar", lambda e, z_=z_, pr=pr: e.copy(out=z_[pr, :, slot, :], in_=pv[pr, :, :]), r=[bk], w=ww, a=aa)
                        else:
                            op("vector", lambda e, z_=z_, pr=pr: e.tensor_copy(out=z_[pr, :, slot, :], in_=pv[pr, :, :]), r=[bk], w=ww, a=aa)
                    return
                ww = [dst] if slot == 0 else []
                aa = [dst] if slot else []
                if eng == "scalar":
                    op("scalar", lambda e: e.copy(out=dst[:, :, slot, :], in_=pv), r=[bk], w=ww, a=aa)
                else:
                    op("vector", lambda e: e.tensor_copy(out=dst[:, :, slot, :], in_=pv), r=[bk], w=ww, a=aa)
            kb.transpose8(ev, src)
        dump(8, BKT[:, :, 0, :] if False else Kt)
        Yt = F[0]
        op("gpsimd", lambda e: e.memset(Yt[:, 0:1], 0.0), w=[Yt])

        def hgroup(hg):
            s_ = hg % 2
            MI, MK, MT, Nk, NkT, Tk, G, Stmp = MIs[s_], MKs[s_], MTs[s_], Nks[s_], NkTs[s_], Tks[s_], Gs[s_], Stmps[s_]
            WT, Ab, RbT, Pp, St, Sb = WTh[hg], Abh[hg], RbTh[hg], Pph[hg], Sth[hg], Sbh[hg]
            bkY = kb.ps[5 + s_]
            heads = [4 * hg + i for i in range(4)]
            for which, dst, mask in ((0, MI, kb.mask_si), (1, MK, kb.mask_si), (2, MT, kb.mask_tt)):
                dst5 = dst[:, :, :, :].rearrange("p (pl par) a t -> p pl par a t", par=2)
                for ip, par in enumerate((0, 1) if which % 2 == 0 else (1, 0)):
                    bk = kb.bank()
                    for pl in range(2):
                        h = heads[2 * pl + par]
                        p, b0 = h // 2, 64 * (h % 2)
                        z_ = ARTz[par]
                        if which == 0:
                            l, rr_ = BKT[:, p, 0, :], z_[:, p, :, :]
                        elif which == 1:
                            l, rr_ = BKT[:, p, 1, :], z_[:, p, :, :]
                        else:
                            l, rr_ = z_[:, p, 0, :], BKT[:, p, :, :]
                        op("tensor", lambda e, bk=bk, l=l, rr_=rr_, pl=pl: e.matmul(
                            bk[:, 256 * pl:256 * pl + 256], lhsT=l, rhs=rr_.rearrange("p a t -> p (a t)"),
                            start=True, stop=True), r=[z_, BKT], w=[bk] if pl == 0 else [], a=[bk] if pl else [])
                    op("vector", lambda e, bk=bk, dst5=dst5, mask=mask, par=par: e.tensor_tensor(
                        out=dst5[:, :, par, :, :].rearrange("p pl a t -> p pl (a t)"),
                        in0=bk[:, :].rearrange("p (pl x) -> p pl x", pl=2),
                        in1=mask[:, :].rearrange("p (pl x) -> p pl x", pl=2),
                        op=ALU.mult), r=[bk, mask], w=[dst] if ip == 0 else [], a=[dst] if ip else [])
                yield
            identb = kb.ident[:, :].unsqueeze(1).broadcast_to([128, 4, 128])
            op("vector", lambda e: e.tensor_tensor(out=Tk[0][:, :, :], in0=MI[:, :, 0, :], in1=identb, op=ALU.add),
               r=[MI, kb.ident], w=[Tk[0]])
            curN = lambda i: MI[:, i, 0, :]
            curNT = lambda i: MT[:, i, 0, :]
            curN_t, curNT_t = MI, MT
            for lv in range(1, 6):
                nN, nNT = Nk[lv % 2], NkT[lv % 2]
                if lv < 5:
                    bk = kb.bank()
                    for i in range(4):
                        op("tensor", lambda e, bk=bk, i=i, a_=curNT(i), b_=curN(i): e.matmul(
                            bk[:, 128 * i:128 * i + 128], lhsT=a_, rhs=b_, start=True, stop=True),
                           r=[curN_t, curNT_t], w=[bk] if i == 0 else [], a=[bk] if i else [])
                    op("scalar", lambda e, bk=bk, nN=nN: e.copy(out=nN[:, :, :].rearrange("p h t -> p (h t)"), in_=bk[:, :]),
                       r=[bk], w=[nN])
                bk2 = kb.bank()
                for i in range(4):
                    op("tensor", lambda e, bk2=bk2, i=i, a_=curN(i), b_=curNT(i): e.matmul(
                        bk2[:, 128 * i:128 * i + 128], lhsT=a_, rhs=b_, start=True, stop=True),
                       r=[curN_t, curNT_t], w=[bk2] if i == 0 else [], a=[bk2] if i else [])
                op("scalar", lambda e, bk2=bk2, nNT=nNT: e.copy(out=nNT[:, :, :].rearrange("p h t -> p (h t)"), in_=bk2[:, :]),
                   r=[bk2], w=[nNT])
                yield
                To, Tn = Tk[(lv - 1) % 2], Tk[lv % 2]
                bk3 = kb.bank()
                for i in range(4):
                    op("tensor", lambda e, bk3=bk3, i=i, nNT=nNT, To=To: e.matmul(
                        bk3[:, 128 * i:128 * i + 128], lhsT=nNT[:, i, :], rhs=To[:, i, :], start=True, stop=True),
                       r=[nNT, To], w=[bk3] if i == 0 else [], a=[bk3] if i else [])
                op("vector", lambda e, bk3=bk3, To=To, Tn=Tn: e.tensor_tensor(
                    out=Tn[:, :, :].rearrange("p h t -> p (h t)"), in0=bk3[:, :],
                    in1=To[:, :, :].rearrange("p h t -> p (h t)"), op=ALU.add), r=[bk3, To], w=[Tn])
                yield
                curN = lambda i, nN=nN: nN[:, i, :]
                curNT = lambda i, nNT=nNT: nNT[:, i, :]
                curN_t, curNT_t = nN, nNT
            Tf = Tk[5 % 2]
            bk = kb.bank()
            for i in range(4):
                op("tensor", lambda e, bk=bk, i=i: e.matmul(bk[:, 128 * i:128 * i + 128], lhsT=MT[:, i, 1, :], rhs=Tf[:, i, :],
                                                            start=True, stop=True),
                   r=[MT, Tf], w=[bk] if i == 0 else [], a=[bk] if i else [])
            op("scalar", lambda e, bk=bk: e.copy(out=G[:, :, :].rearrange("p h t -> p (h t)"), in_=bk[:, :]), r=[bk], w=[G])
            yield
            bk = kb.bank()
            for i in range(4):
                h = heads[i]
                op("tensor", lambda e, bk=bk, i=i, h=h: e.matmul(bk[:, 64 * i:64 * i + 64], lhsT=G[:, i, :],
                                                                 rhs=Vb[:, 64 * h:64 * h + 64], start=True, stop=True),
                   r=[G, Vb], w=[bk] if i == 0 else [], a=[bk] if i else [])
                op("tensor", lambda e, bk=bk, i=i, h=h: e.matmul(bk[:, 256 + 64 * i:256 + 64 * i + 64], lhsT=Tf[:, i, :],
                                                                 rhs=At[:, 64 * h:64 * h + 64], start=True, stop=True),
                   r=[Tf, At], a=[bk])
            hs = slice(256 * hg, 256 * hg + 256)
            op("scalar", lambda e, bk=bk: e.copy(out=WT[:, :], in_=bk[:, 0:256]), r=[bk], w=[WT])
            op("scalar", lambda e, bk=bk: e.copy(out=Ab[:, :], in_=bk[:, 256:512]), r=[bk], w=[Ab])
            yield
            bk = kb.bank()
            for i in range(4):
                h = heads[i]
                b0 = 64 * (h % 2)
                pl = i // 2
                op("tensor", lambda e, bk=bk, i=i, b0=b0, pl=pl: e.matmul(
                    bk[b0:b0 + 64, 128 * pl:128 * pl + 128], lhsT=Ab[:, 64 * i:64 * i + 64], rhs=MI[:, i, 1, :],
                    start=True, stop=True), r=[Ab, MI], w=[bk] if i == 0 else [], a=[bk] if i else [])
            for par in range(2):
                pr = slice(64 * par, 64 * par + 64)
                z_ = ARTz[par]
                op("vector", lambda e, bk=bk, pr=pr, z_=z_: e.tensor_tensor(
                    out=RbT[pr, :, :], in0=bk[pr, 0:256].rearrange("p (a t) -> p a t", t=128),
                    in1=z_[pr, 2 * hg:2 * hg + 2, 1, :], op=ALU.add), r=[bk, z_],
                   w=[RbT] if par == 0 else [], a=[RbT] if par else [])
            bk = kb.bank()
            firstP = True
            for ch in range(2):
                for i in range(4):
                    h = heads[i]
                    b0 = 64 * (h % 2)
                    pl = i // 2
                    col = (pl * 2 + ch) * 64
                    op("tensor", lambda e, bk=bk, h=h, i=i, b0=b0, ch=ch, col=col: e.matmul(
                        bk[b0:b0 + 64, col:col + 64], lhsT=Ab[64 * ch:64 * ch + 64, 64 * i:64 * i + 64],
                        rhs=Bg[64 * ch:64 * ch + 64, 64 * h:64 * h + 64], start=True, stop=True),
                       r=[Ab, Bg], w=[bk] if firstP else [], a=[] if firstP else [bk])
                    firstP = False
            op("scalar", lambda e, bk=bk: e.copy(
                out=Pp[:, :, :, :].rearrange("p a c j -> p (a c j)"), in_=bk[:, 0:256]), r=[bk], w=[Pp])
            yield
            firstY = True
            for i in range(4):
                h = heads[i]
                for lhs_t, rhs_t, rhs_ap in ((MI, WT, WT[:, 64 * i:64 * i + 64]), (MK, Vb, Vb[:, 64 * h:64 * h + 64])):
                    op("tensor", lambda e, i=i, lhs_t=lhs_t, rhs_ap=rhs_ap, firstY=firstY: e.matmul(
                        bkY[:, 64 * i:64 * i + 64], lhsT=lhs_t[:, i, 1, :], rhs=rhs_ap,
                        start=firstY, stop=False, skip_group_check=True), r=[lhs_t, rhs_t],
                       w=[bkY] if firstY else [], a=[] if firstY else [bkY])
                    firstY = False
            yield
            for ch in range(2):
                for i in (0, 2, 1, 3):
                    h = heads[i]
                    pl, b0 = i // 2, 64 * (h % 2)
                    op("tensor", lambda e, i=i, pl=pl, b0=b0, ch=ch: e.matmul(
                        bkY[64 * ch:64 * ch + 64, 64 * i:64 * i + 64], lhsT=RbT[b0:b0 + 64, pl, 64 * ch:64 * ch + 64],
                        rhs=Sb[b0:b0 + 64, pl, :], start=False, stop=(ch == 1), skip_group_check=True),
                       r=[RbT, Sb], a=[bkY])
                bkS = kb.bank()
                seen_par = set()
                firstS = True
                outvs = {}
                for i in (0, 2, 1, 3):
                    h = heads[i]
                    pl, b0 = i // 2, 64 * (h % 2)
                    outv = bkS[b0:b0 + 64, 64 * pl:64 * pl + 64]
                    outvs[i] = outv
                    st_ = (b0 not in seen_par)
                    seen_par.add(b0)
                    op("tensor", lambda e, outv=outv, pl=pl, b0=b0, ch=ch, st_=st_: e.matmul(
                        outv, lhsT=Pp[b0:b0 + 64, pl, ch, :], rhs=St[b0:b0 + 64, pl, :], start=st_, stop=False,
                        skip_group_check=True), r=[Pp, St], w=[bkS] if firstS else [], a=[] if firstS else [bkS])
                    firstS = False
                for i in range(4):
                    h = heads[i]
                    outv = outvs[i]
                    op("tensor", lambda e, outv=outv, h=h, i=i, ch=ch: e.matmul(
                        outv, lhsT=Bg[64 * ch:64 * ch + 64, 64 * h:64 * h + 64], rhs=WT[64 * ch:64 * ch + 64, 64 * i:64 * i + 64],
                        start=False, stop=False, skip_group_check=True), r=[Bg, WT], a=[bkS])
                    op("tensor", lambda e, outv=outv, h=h, ch=ch: e.matmul(
                        outv, lhsT=Kg[64 * ch:64 * ch + 64, 64 * h:64 * h + 64], rhs=Vb[64 * ch:64 * ch + 64, 64 * h:64 * h + 64],
                        start=False, stop=True, skip_group_check=True), r=[Kg, Vb], a=[bkS])
                op("vector", lambda e, ch=ch: e.tensor_tensor(
                    out=Stmp[:, :, :], in0=St[:, :, :], in1=gCc[:, 2 * hg:2 * hg + 2, ch:ch + 1].broadcast_to([128, 2, 64]),
                    op=ALU.mult), r=[St, gCc], w=[Stmp])
                op("vector", lambda e, bkS=bkS: e.tensor_tensor(
                    out=St[:, :, :], in0=bkS[:, 0:128].rearrange("p (a i) -> p a i", i=64), in1=Stmp[:, :, :],
                    op=ALU.add), r=[bkS, Stmp], w=[St])
                op("scalar", lambda e: e.copy(out=Sb[:, :, :], in_=St[:, :, :]), r=[St], w=[Sb])
                yield
            op("scalar", lambda e, hs=hs: e.copy(out=Yt[:, hs], in_=bkY[:, 0:256]), r=[bkY], a=[Yt])
            yield

        def chain2(a_, b_):
            yield from a_
            yield from b_
        gens = [chain2(hgroup(0), hgroup(2)), chain2(hgroup(1), hgroup(3))]
        if it + 1 < NT:
            gens.append(part1_g(it + 1))
        while gens:
            for g_ in list(gens):
                try:
                    next(g_)
                except StopIteration:
                    gens.remove(g_)
        dump(11, Yt, True)
        sq = F[1]
        op("vector", lambda e: e.tensor_reduce(out=sm[2][:, :], in_=h3(Yt), axis=AX.X, op=ALU.add), r=[Yt], w=[sm[2]])
        op("gpsimd", lambda e: e.tensor_tensor(out=sq[:, :], in0=Yt[:, :], in1=Yt[:, :], op=ALU.mult), r=[Yt], w=[sq])
        op("vector", lambda e: e.tensor_reduce(out=sm[3][:, :], in_=h3(sq), axis=AX.X, op=ALU.add), r=[sq], w=[sm[3]])
        mean, var = sm[2], sm[3]
        op("vector", lambda e: e.tensor_scalar_mul(out=mean[:, :], in0=mean[:, :], scalar1=1.0 / 64), r=[mean], w=[mean])
        op("vector", lambda e: e.tensor_tensor(out=sm[4][:, :], in0=mean[:, :], in1=mean[:, :], op=ALU.mult), r=[mean], w=[sm[4]])
        op("vector", lambda e: e.scalar_tensor_tensor(out=var[:, :], in0=var[:, :], scalar=1.0 / 64, in1=sm[4][:, :],
                                                      op0=ALU.mult, op1=ALU.subtract), r=[var, sm[4]], w=[var])
        op("scalar", lambda e: e.activation(out=var[:, :], in_=var[:, :], func=AF.Sqrt, bias=GN_EPS, scale=1.0), r=[var], w=[var])
        op("vector", lambda e: e.reciprocal(out=var[:, :], in_=var[:, :]), r=[var], w=[var])
        yn = F[1]
        op("vector", lambda e: e.tensor_tensor(out=h3(yn), in0=h3(Yt), in1=bc3(mean), op=ALU.subtract), r=[Yt, mean], w=[yn])
        op("vector", lambda e: e.tensor_tensor(out=h3(yn), in0=h3(yn), in1=bc3(var), op=ALU.mult), r=[yn, var], w=[yn])
        op("vector", lambda e: e.tensor_tensor(out=yn[:, :], in0=yn[:, :], in1=bc_lg[:, :], op=ALU.mult), r=[yn, bc_lg], w=[yn])
        op("vector", lambda e: e.tensor_tensor(out=yn[:, :], in0=yn[:, :], in1=bc_lb[:, :], op=ALU.add), r=[yn, bc_lb], w=[yn])
        bv = F[3]
        op("vector", lambda e: e.tensor_tensor(out=h3(bv), in0=h3(Vb), in1=bc3(bsum), op=ALU.mult), r=[Vb, bsum], w=[bv])
        op("vector", lambda e: e.tensor_tensor(out=yn[:, :], in0=yn[:, :], in1=bv[:, :], op=ALU.add), r=[yn, bv], w=[yn])
        op("vector", lambda e: e.tensor_tensor(out=ygt[:, :], in0=yn[:, :], in1=sgb[:, :], op=ALU.mult), r=[yn, sgb], w=[ygt])
        return kb.dma("sync", yg_out[it * 128:(it + 1) * 128, :], ygt[:, :], r=[ygt], key="styg")
    last = None
    try:
        for it_ in range(NT):
            last = tile(it_)
    except Stop:
        last = kb.last
    kb.mem.reset(m0)
    kb.rot = [0, 1, 2, 3, 4, 5, 6]
    return last


LAM_INIT = 0.8 - 0.6 * math.exp(-0.3 * 1)


class TV:
    def __init__(self, ap, buf):
        self.ap = ap
        self.buf = buf

    def __getitem__(self, k):
        return self.ap[k]


class PM:
    def __init__(self, kb, dram, l, wout_name, stg=None):
        al = kb.alloc
        self.kb = kb
        self.dram, self.l, self.wout_name = dram, l, wout_name
        self.Wout = al("Wout", [128, 8, 1024], BF16)
        self.Wg = al("Wg", [128, 8, 1024], BF16)
        self.Wp = al("Wp", [128, 2, 1024], BF16)
        self.bc_post = al("bc_post", [128, 1024], BF16)
        self.bc_ple = al("bc_ple", [128, 1024], BF16)
        if stg is not None:
            self.load(stg)

    def load(self, stg):
        kb, dram, l, wout_name = self.kb, self.dram, self.l, self.wout_name
        kb.load_w_bf16(self.Wout, dram[wout_name], 1024, stg)
        kb.load_w_bf16(self.Wg, dram["ple_gate"][l], 1024, stg)
        kb.load_w_bf16(self.Wp, dram["ple_w"][l], 1024, stg, nck=2)
        kb.load_bcast(self.bc_post, dram["norm_post"][l], stg=stg[0], key="bcA")
        kb.load_bcast(self.bc_ple, dram["ple_norm"][l], stg=stg[1], key="bcB")

    def alloc_tmps(self):
        al = self.kb.alloc

        class TM:
            pass
        tm = TM()
        tm.ygT = al("ygT", [128, 8, 128], BF16)
        tm.ot = al("ot", [128, 1024], F32)
        tm.tmp = al("pm_tmp", [128, 1024], F32)
        tm.hmid = al("hmid", [128, 1024], F32)
        tm.pb = al("pb", [128, 256], BF16)
        tm.pT = al("pT", [128, 2, 128], BF16)
        tm.e_sb = al("e_sb", [128, 1024], F32)
        tm.hb = al("hb", [128, 1024], BF16)
        tm.hT = tm.ygT
        tm.ge = al("ge", [128, 1024], F32)
        tm.st = al("pm_st", [128, 4], F32)
        tm.junk = TV(tm.tmp.h[:, :].bitcast(BF16)[:, 0:1024], tm.tmp.buf)
        return tm

    def rstd_from(self, tm, srcT, src, col, eps, scale_sq, n=1024):
        kb, st, junk = self.kb, tm.st, tm.junk
        kb.op("gpsimd", lambda e: e.memset(st[:, col:col + 1], 0.0), w=[st])
        kb.op("scalar", lambda e: e.activation(out=junk[:, 0:n], in_=src, func=AF.Square, scale=scale_sq,
                                               accum_out=st[:, col:col + 1]), r=[srcT, st], w=[junk, st])
        kb.op("scalar", lambda e: e.activation(out=st[:, col:col + 1], in_=st[:, col:col + 1], func=AF.Sqrt, bias=eps,
                                               scale=1.0), r=[st], w=[st])
        kb.op("vector", lambda e: e.reciprocal(out=st[:, col:col + 1], in_=st[:, col:col + 1]), r=[st], w=[st])

    def run(self, ygt, resid, pt, hout, tm):
        for _ in self.run_g(ygt, resid, pt, hout, tm):
            pass

    def run_g(self, ygt, resid, pt, hout, tm):
        kb = self.kb
        op = kb.op
        ygT, ot, tmp, hmid, pb, pT, e_sb, hb, hT, ge, st = (tm.ygT, tm.ot, tm.tmp, tm.hmid, tm.pb, tm.pT,
                                                            tm.e_sb, tm.hb, tm.hT, tm.ge, tm.st)
        kb.transpose8(lambda bk, pv: op("scalar", lambda e: e.copy(out=ygT[:, :, :], in_=pv), r=[bk], w=[ygT]), ygt)
        yield
        for half in range(2):
            bk = kb.bank()
            cs = slice(half * 512, half * 512 + 512)
            for ck in range(8):
                op("tensor", lambda e, bk=bk, ck=ck, cs=cs: e.matmul(bk[:, :], lhsT=ygT[:, ck, :], rhs=self.Wout[:, ck, cs],
                                                                     start=(ck == 0), stop=(ck == 7)),
                   r=[ygT, self.Wout], w=[bk] if ck == 0 else [], a=[bk] if ck else [])
            op("vector", lambda e, bk=bk, cs=cs: e.tensor_copy(out=ot[:, cs], in_=bk[:, :]), r=[bk],
               w=[ot] if half == 0 else [], a=[ot] if half else [])
        yield
        self.rstd_from(tm, ot, ot[:, :], 0, NORM_EPS, 1.0 / 32)
        op("gpsimd", lambda e: e.tensor_copy(out=pb[:, :], in_=pt[:, :]), r=[pt], w=[pb])
        kb.transpose8(lambda bk, pv: op("scalar", lambda e: e.copy(out=pT[:, :, :], in_=pv[:, 0:2, :]), r=[bk], w=[pT]),
                      pb, nck=2)
        yield
        op("vector", lambda e: e.scalar_tensor_tensor(out=tmp[:, :], in0=ot[:, :], scalar=st[:, 0:1], in1=self.bc_post[:, :],
                                                      op0=ALU.mult, op1=ALU.mult), r=[ot, st, self.bc_post], w=[tmp])
        op("vector", lambda e: e.tensor_tensor(out=hmid[:, :], in0=tmp[:, :], in1=resid[:, :], op=ALU.add),
           r=[tmp, resid], w=[hmid])
        for half in range(2):
            bk = kb.bank()
            cs = slice(half * 512, half * 512 + 512)
            for ck in range(2):
                op("tensor", lambda e, bk=bk, ck=ck, cs=cs: e.matmul(bk[:, :], lhsT=pT[:, ck, :], rhs=self.Wp[:, ck, cs],
                                                                     start=(ck == 0), stop=(ck == 1)),
                   r=[pT, self.Wp], w=[bk] if ck == 0 else [], a=[bk] if ck else [])
            op("scalar", lambda e, bk=bk, cs=cs: e.copy(out=e_sb[:, cs], in_=bk[:, :]), r=[bk],
               w=[e_sb] if half == 0 else [], a=[e_sb] if half else [])
        yield
        op("scalar", lambda e: e.copy(out=hb[:, :], in_=hmid[:, :]), r=[hmid], w=[hb])
        kb.transpose8(lambda bk, pv: op("vector", lambda e: e.tensor_copy(out=hT[:, :, :], in_=pv), r=[bk], w=[hT]), hb)
        yield
        for half in range(2):
            bk = kb.bank()
            cs = slice(half * 512, half * 512 + 512)
            for ck in range(8):
                op("tensor", lambda e, bk=bk, ck=ck, cs=cs: e.matmul(bk[:, :], lhsT=hT[:, ck, :], rhs=self.Wg[:, ck, cs],
                                                                     start=(ck == 0), stop=(ck == 7)),
                   r=[hT, self.Wg], w=[bk] if ck == 0 else [], a=[bk] if ck else [])
            op("scalar", lambda e, bk=bk, cs=cs: e.activation(out=ge[:, cs], in_=bk[:, :], func=AF.Sigmoid), r=[bk],
               w=[ge] if half == 0 else [], a=[ge] if half else [])
        yield
        op("vector", lambda e: e.tensor_tensor(out=ge[:, :], in0=ge[:, :], in1=e_sb[:, :], op=ALU.mult), r=[ge, e_sb], w=[ge])
        self.rstd_from(tm, ge, ge[:, :], 1, NORM_EPS, 1.0 / 32)
        yield
        op("vector", lambda e: e.scalar_tensor_tensor(out=tmp[:, :], in0=ge[:, :], scalar=st[:, 1:2], in1=self.bc_ple[:, :],
                                                      op0=ALU.mult, op1=ALU.mult), r=[ge, st, self.bc_ple], w=[tmp])
        op("vector", lambda e: e.tensor_tensor(out=hout[:, :], in0=tmp[:, :], in1=hmid[:, :], op=ALU.add),
           r=[tmp, hmid], w=[hout])
        yield


def interleave(gens):
    gens = list(gens)
    while gens:
        for g_ in list(gens):
            try:
                next(g_)
            except StopIteration:
                gens.remove(g_)


def rope_ops(kb, src, dst, cs_t, tmp, it):
    op = kb.op
    v3 = lambda t: t[:, :].rearrange("p (g d) -> p g d", d=64)
    cosb = cs_t[:, 0, :].unsqueeze(1).broadcast_to([128, 16, 8])
    sinb = cs_t[:, 1, :].unsqueeze(1).broadcast_to([128, 16, 8])
    t4 = tmp[:, 0:512].rearrange("p (k g d) -> p k g d", k=4, d=8)
    op("scalar", lambda e: e.copy(out=dst[:, :], in_=src[:, :]), r=[src], w=[dst])
    x1, x2 = v3(src)[:, :, 0:8], v3(src)[:, :, 8:16]
    op("vector", lambda e: e.tensor_tensor(out=t4[:, 0], in0=x1, in1=cosb, op=ALU.mult), r=[src, cs_t], w=[tmp])
    op("vector", lambda e: e.tensor_tensor(out=t4[:, 1], in0=x2, in1=sinb, op=ALU.mult), r=[src, cs_t], a=[tmp])
    op("vector", lambda e: e.tensor_tensor(out=t4[:, 2], in0=x2, in1=cosb, op=ALU.mult), r=[src, cs_t], a=[tmp])
    op("vector", lambda e: e.tensor_tensor(out=t4[:, 3], in0=x1, in1=sinb, op=ALU.mult), r=[src, cs_t], a=[tmp])
    op("vector", lambda e: e.tensor_tensor(out=v3(dst)[:, :, 0:8], in0=t4[:, 0], in1=t4[:, 1], op=ALU.subtract),
       r=[tmp], a=[dst])
    op("vector", lambda e: e.tensor_tensor(out=v3(dst)[:, :, 8:16], in0=t4[:, 2], in1=t4[:, 3], op=ALU.add),
       r=[tmp], a=[dst])


def phase_a2(kb, dram, yg_in, h1_out, kt_out, v_out):
    nc, S = kb.nc, kb.S
    al, op = kb.alloc, kb.op
    NT = S // 128
    kb.c.barrier()
    m0 = kb.mem.mark()
    Wkv = al("Wkv", [128, 8, 2048], BF16)
    gkv_fm = al("gkv_fm", [128, 1, 8], F32)
    pm = PM(kb, dram, 0, "a_w_out")
    mstg = kb.mem.mark()
    stg = [al(f"wstgB{i}", [128, 1024], F32) for i in range(3)]
    pm.load(stg)
    kb.load_w_bf16(Wkv, dram["kv_w"], 2048, stg)
    kb.load_fm_vecs(gkv_fm, [dram["kv_norm"]], stg[2])
    kb.c.barrier()
    kb.mem.reset(mstg)
    tms = [pm.alloc_tmps() for _ in range(2)]
    NI = 4
    xt = [al(f"xtA{i}", [128, 1024], F32) for i in range(NI)]
    ygt = [al(f"ygtA{i}", [128, 1024], BF16) for i in range(NI)]
    ptl = [al(f"ptA{i}", [128, 256], F32) for i in range(NI)]
    cst = [al(f"cst{i}", [128, 2, 8], F32) for i in range(NI)]
    h1s = [al(f"h1_{i}", [128, 1024], F32) for i in range(2)]
    Kbs = [al(f"Kb{i}", [128, 1024], BF16) for i in range(2)]
    KTss = [al(f"KTsb{i}", [128, 8, 128], BF16) for i in range(2)]
    Vts = [al(f"Vt{i}", [128, 1024], BF16) for i in range(2)]
    print("A2 sbuf peak", kb.mem.peak)

    def loads(it):
        b = it % NI
        sl = slice(it * 128, (it + 1) * 128)
        kb.dma("sync", xt[b][:, :], dram["x"][sl, :], w=[xt[b]], key=f"a2x{b}")
        kb.dma("sync", ygt[b][:, :], yg_in[sl, :], w=[ygt[b]], key=f"a2y{b}")
        kb.dma("sync", ptl[b][:, :], dram["p"][0, sl, :], w=[ptl[b]], key=f"a2p{b}")
        kb.dma("sync", cst[b][:, 0, :], dram["c_cos"][sl, :], w=[cst[b]], key=f"a2c{b}")
        kb.dma("sync", cst[b][:, 1, :], dram["c_sin"][sl, :], a=[cst[b]], key=f"a2c{b}")

    lasts = []

    def tile_g(it, P):
        b = it % NI
        sl = slice(it * 128, (it + 1) * 128)
        tm = tms[P]
        h1, Kb, KTs, Vt = h1s[P], Kbs[P], KTss[P], Vts[P]
        hk, hkT, Kf, rtmp = tm.hb, tm.hT, tm.ot, tm.tmp
        yield from pm.run_g(ygt[b], xt[b], ptl[b], h1, tm)
        kb.dma("sync", h1_out[sl, :], h1[:, :], r=[h1], key=f"sth1_{P}")
        pm.rstd_from(tm, h1, h1[:, :], 2, NORM_EPS, 1.0 / 32)
        yield
        op("vector", lambda e: e.tensor_scalar(out=hk[:, :], in0=h1[:, :], scalar1=tm.st[:, 2:3], scalar2=None, op0=ALU.mult),
           r=[h1, tm.st], w=[hk])
        kb.transpose8(lambda bk, pv: op("vector", lambda e: e.tensor_tensor(
            out=hkT[:, :, :], in0=pv, in1=gkv_fm[:, 0, :].unsqueeze(2).broadcast_to([128, 8, 128]), op=ALU.mult),
            r=[bk, gkv_fm], w=[hkT]), hk)
        yield
        for which in range(2):
            for half in range(2):
                bk = kb.bank()
                c0 = which * 1024 + half * 512
                cs = slice(half * 512, half * 512 + 512)
                for ck in range(8):
                    op("tensor", lambda e, bk=bk, ck=ck, c0=c0: e.matmul(bk[:, :], lhsT=hkT[:, ck, :], rhs=Wkv[:, ck, c0:c0 + 512],
                                                                         start=(ck == 0), stop=(ck == 7)),
                       r=[hkT, Wkv], w=[bk] if ck == 0 else [], a=[bk] if ck else [])
                dstt = Kf if which == 0 else Vt
                op("scalar", lambda e, bk=bk, cs=cs, dstt=dstt: e.copy(out=dstt[:, cs], in_=bk[:, :]), r=[bk],
                   w=[dstt] if half == 0 else [], a=[dstt] if half else [])
            yield
        rope_ops(kb, Kf, Kb, cst[b], rtmp, it)
        yield
        kb.transpose8(lambda bk, pv: op("scalar", lambda e: e.copy(out=KTs[:, :, :], in_=pv), r=[bk], w=[KTs]), Kb)
        kb.dma("sync", kt_out[:, :, sl].rearrange("h p t -> p h t"), KTs[:, :, :], r=[KTs], key=f"stkt_{P}")
        lasts.append(kb.dma("sync", v_out[sl, :], Vt[:, :], r=[Vt], key=f"stv_{P}"))
        yield

    loads(0)
    if NT > 1:
        loads(1)
    for it in range(0, NT, 2):
        for nx in (it + 2, it + 3):
            if nx < NT:
                loads(nx)
        gens = [tile_g(it, 0)]
        if it + 1 < NT:
            gens.append(tile_g(it + 1, 1))
        interleave(gens)
    kb.mem.reset(m0)
    return lasts[-1]


def phase_b(kb, dram, h1_in, kt_in, v_in, out):
    nc, S = kb.nc, kb.S
    al, op = kb.alloc, kb.op
    NG = S // 256
    CH = 8
    kb.c.barrier()
    m0 = kb.mem.mark()
    Win = al("WinB", [128, 8, 2048], BF16)
    g1_fm = al("g1_fm", [128, 1, 8], F32)
    bc_sub = al("bc_sub", [128, 128], F32)
    lq = al("lq", [128, 256], F32)
    lam = al("lam", [128, 4], F32)
    pm = PM(kb, dram, 1, "b_w_out")
    mstg = kb.mem.mark()
    stg = [al(f"wstgC{i}", [128, 1024], F32) for i in range(3)]
    pm.load(stg)
    kb.load_w_bf16(Win, dram["b_w_in"], 2048, stg)
    kb.load_fm_vecs(g1_fm, [dram["norm_pre"][1]], stg[2])
    kb.dma("sync", bc_sub[:, :], dram["b_subln"].partition_broadcast(128), w=[bc_sub], key="bsub")
    op("vector", lambda e: e.tensor_scalar_mul(out=bc_sub[:, :], in0=bc_sub[:, :], scalar1=1.0 - LAM_INIT), r=[bc_sub], w=[bc_sub])
    kb.dma("sync", lq[:, :], dram["b_lambda"].rearrange("a b -> (a b)").partition_broadcast(128), w=[lq], key="blam")
    lq3 = lq[:, :].rearrange("p (a b) -> p a b", b=64)
    op("vector", lambda e: e.tensor_tensor(out=lq3[:, 0, :], in0=lq3[:, 0, :], in1=lq3[:, 1, :], op=ALU.mult), r=[lq], w=[lq])
    op("vector", lambda e: e.tensor_tensor(out=lq3[:, 2, :], in0=lq3[:, 2, :], in1=lq3[:, 3, :], op=ALU.mult), r=[lq], w=[lq])
    op("vector", lambda e: e.tensor_reduce(out=lam[:, 0:1], in_=lq3[:, 0, :], axis=AX.X, op=ALU.add), r=[lq], w=[lam])
    op("vector", lambda e: e.tensor_reduce(out=lam[:, 1:2], in_=lq3[:, 2, :], axis=AX.X, op=ALU.add), r=[lq, lam], w=[lam])
    op("scalar", lambda e: e.activation(out=lam[:, 0:2], in_=lam[:, 0:2], func=AF.Exp), r=[lam], w=[lam])
    op("vector", lambda e: e.tensor_tensor(out=lam[:, 2:3], in0=lam[:, 1:2], in1=lam[:, 0:1], op=ALU.subtract), r=[lam], w=[lam])
    op("vector", lambda e: e.tensor_scalar_add(out=lam[:, 3:4], in0=lam[:, 2:3], scalar1=-LAM_INIT), r=[lam], w=[lam])
    neglam = lam[:, 3:4]
    kb.c.barrier()
    kb.mem.reset(mstg)
    tmB = pm.alloc_tmps()
    h1t = [[al(f"h1t{i}_{j}", [128, 1024], F32) for j in range(2)] for i in range(3)]
    ptl = [[al(f"ptB{i}_{j}", [128, 256], F32) for j in range(2)] for i in range(3)]
    cst = [[al(f"cstB{i}_{j}", [128, 2, 8], F32) for j in range(2)] for i in range(3)]
    xs = tmB.junk
    xnT = al("xnTB", [128, 8, 128], BF16)
    Qb = tmB.hb
    QTp = [al(f"QTp{i}", [128, 8, 2, 256], BF16) for i in range(2)]
    sgb = [[al(f"sgbB{i}_{j}", [128, 1024], BF16) for j in range(2)] for i in range(2)]
    og = [[al(f"og{i}_{j}", [128, 1024], BF16) for j in range(2)] for i in range(2)]
    PT = [al(f"PT{i}", [128, 2, 256], BF16) for i in range(3)]
    NB = 3
    KTc = [al(f"KTc{i}", [128, CH * 128], BF16) for i in range(NB)]
    Vc = [al(f"Vc{i}", [128, CH, 129], BF16) for i in range(NB)]
    t02 = [al(f"t0_{i}", [128, 128], F32) for i in range(2)]
    ob2 = [al(f"ob_{i}", [128, 128], F32) for i in range(2)]
    rr2 = [al(f"rr_{i}", [128, 4], F32) for i in range(2)]
    ast = al("ast", [128, 4], F32)
    ajunk = al("ajunk", [128, 128], BF16)
    hout = al("houtB", [128, 1024], F32)
    rtmp = tmB.tmp
    Qf = tmB.ot
    print("B sbuf peak", kb.mem.peak)
    for V_ in Vc:
        op("vector", lambda e, V_=V_: e.memset(V_[:, :, 128:129], 1.0), w=[V_])
    for Q_ in QTp:
        op("gpsimd", lambda e, Q_=Q_: e.memset(Q_[:, :, :, :], 0.0), w=[Q_])
    kb.rot = [7]
    kb.rr = 0
    abanks = [kb.ps[4], kb.ps[5], kb.ps[6]]
    acnt = [0]
    acc = [[kb.ps[2 * qt + c] for c in range(2)] for qt in range(2)]
    st = tmB.st

    chunks = []
    for g in range(NG):
        n = 2 * g + 2
        for h in range(8):
            for c0 in range(0, n, CH):
                chunks.append((g, h, c0, min(CH, n - c0)))
    cidx = {(g, h, c0): i for i, (g, h, c0, m) in enumerate(chunks)}
    issued = [0]

    def issue_chunks(upto):
        while issued[0] < min(upto, len(chunks)):
            i = issued[0]
            g_, h_, c0, m = chunks[i]
            b = i % NB
            kb.dma("sync", KTc[b][:, 0:m * 128], kt_in[h_, :, c0 * 128:(c0 + m) * 128], w=[KTc[b]], key=f"ldk{b}")
            kb.dma("sync", Vc[b][:, 0:m, 0:128],
                   v_in[c0 * 128:(c0 + m) * 128, h_ * 128:(h_ + 1) * 128].rearrange("(n p) d -> p n d", p=128),
                   w=[Vc[b]], key=f"ldv{b}")
            issued[0] += 1

    def loads(g):
        s3 = g % 3
        for j in range(2):
            it = 2 * g + j
            sl = slice(it * 128, (it + 1) * 128)
            kb.dma("sync", h1t[s3][j][:, :], h1_in[sl, :], w=[h1t[s3][j]], key=f"bh{s3}{j}")
            kb.dma("sync", ptl[s3][j][:, :], dram["p"][1, sl, :], w=[ptl[s3][j]], key=f"bp{s3}{j}")
            kb.dma("sync", cst[s3][j][:, 0, :], dram["c_cos"][sl, :], w=[cst[s3][j]], key=f"bc{s3}{j}")
            kb.dma("sync", cst[s3][j][:, 1, :], dram["c_sin"][sl, :], a=[cst[s3][j]], key=f"bc{s3}{j}")

    def pre_g(g):
        s3, P = g % 3, g % 2
        QT = QTp[P]
        for j in range(2):
            hj = h1t[s3][j]
            pm.rstd_from(tmB, hj, hj[:, :], 2, NORM_EPS, 1.0 / 32)
            yield
            op("vector", lambda e, hj=hj: e.tensor_scalar(out=xs[:, :], in0=hj[:, :], scalar1=st[:, 2:3], scalar2=None,
                                                         op0=ALU.mult), r=[hj, st], w=[xs])
            kb.transpose8(lambda bk, pv: op("vector", lambda e: e.tensor_tensor(
                out=xnT[:, :, :], in0=pv, in1=g1_fm[:, 0, :].unsqueeze(2).broadcast_to([128, 8, 128]), op=ALU.mult),
                r=[bk, g1_fm], w=[xnT]), xs)
            yield
            for which in range(2):
                for half in range(2):
                    bk = kb.bank()
                    c0 = which * 1024 + half * 512
                    cs = slice(half * 512, half * 512 + 512)
                    for ck in range(8):
                        op("tensor", lambda e, bk=bk, ck=ck, c0=c0: e.matmul(bk[:, :], lhsT=xnT[:, ck, :], rhs=Win[:, ck, c0:c0 + 512],
                                                                             start=(ck == 0), stop=(ck == 7)),
                           r=[xnT, Win], w=[bk] if ck == 0 else [], a=[bk] if ck else [])
                    if which == 0:
                        op("scalar", lambda e, bk=bk, cs=cs: e.mul(out=Qf[:, cs], in_=bk[:, :], mul=0.125),
                           r=[bk], w=[Qf] if half == 0 else [], a=[Qf] if half else [])
                    else:
                        sj = sgb[P][j]
                        op("scalar", lambda e, bk=bk, cs=cs, sj=sj: e.activation(out=sj[:, cs], in_=bk[:, :], func=AF.Silu),
                           r=[bk], w=[sj] if half == 0 else [], a=[sj] if half else [])
                    yield
            rope_ops(kb, Qf, Qb, cst[s3][j], rtmp, 0)
            yield

            def evq(bk, pv, j=j, QT=QT):
                op("scalar", lambda e: e.copy(out=QT[0:64, :, 0, 128 * j:128 * j + 128], in_=pv[0:64, :, :]),
                   r=[bk], w=[QT] if j == 0 else [], a=[QT] if j else [])
                op("scalar", lambda e: e.copy(out=QT[64:128, :, 1, 128 * j:128 * j + 128], in_=pv[64:128, :, :]),
                   r=[bk], a=[QT])
            kb.transpose8(evq, Qb)
            yield

    def attn_g(g):
        P = g % 2
        QT = QTp[P]
        n = 2 * g + 2
        steps = [(h, kt) for h in range(8) for kt in range(n)]
        banks = {}

        def emit_qk(h, kt):
            c0 = (kt // CH) * CH
            ci = cidx[(g, h, c0)]
            if kt == c0:
                issue_chunks(ci + NB - 1)
            KTs = KTc[ci % NB]
            kl = kt - c0
            bk = abanks[acnt[0] % 3]
            acnt[0] += 1
            banks[(h, kt)] = bk
            op("tensor", lambda e, bk=bk, kl=kl, h=h, KTs=KTs: e.matmul(
                bk[:, :], lhsT=KTs[:, kl * 128:(kl + 1) * 128],
                rhs=QT[:, h, :, :].rearrange("p c q -> p (c q)"), start=True, stop=True),
               r=[KTs, QT], w=[bk])

        def emit_rest(h, kt, sidx):
            c0 = (kt // CH) * CH
            ci = cidx[(g, h, c0)]
            Vs = Vc[ci % NB]
            kl = kt - c0
            lastk = (kt == n - 1)
            q0 = 128 if lastk else 0
            bk = banks.pop((h, kt))
            pt = PT[sidx % 3]
            bk3 = bk[:, :].rearrange("p (c q) -> p c q", q=256)
            op("scalar", lambda e, bk3=bk3, pt=pt, q0=q0: e.activation(out=pt[:, :, q0:256], in_=bk3[:, :, q0:256], func=AF.Exp),
               r=[bk], w=[pt])
            if kt >= n - 2:
                qd = 128 * (kt - (n - 2))
                op("gpsimd", lambda e, pt=pt, qd=qd: e.tensor_tensor(
                    out=pt[:, :, qd:qd + 128], in0=pt[:, :, qd:qd + 128],
                    in1=kb.mask_kq[:, :].unsqueeze(1).broadcast_to([128, 2, 128]), op=ALU.mult), r=[pt, kb.mask_kq], w=[pt])
            for qt in range(2):
                if lastk and qt == 0:
                    continue
                lastq = (kt == (n - 2 if qt == 0 else n - 1))
                for c in range(2):
                    ab = acc[qt][c]
                    op("tensor", lambda e, ab=ab, pt=pt, c=c, qt=qt, kl=kl, kt=kt, lastq=lastq, Vs=Vs: e.matmul(
                        ab[:, 0:129], lhsT=pt[:, c, 128 * qt:128 * qt + 128], rhs=Vs[:, kl, :], start=(kt == 0), stop=lastq),
                       r=[pt, Vs], w=[ab] if kt == 0 else [], a=[ab] if kt else [])

        def finalize(h):
            for qt in range(2):
                a0, a1 = acc[qt]
                rr_, t0_, ob_ = rr2[qt], t02[qt], ob2[qt]
                op("vector", lambda e, a0=a0, rr_=rr_: e.reciprocal(out=rr_[:, 0:1], in_=a0[:, 128:129]), r=[a0], w=[rr_])
                op("vector", lambda e, a1=a1, rr_=rr_: e.reciprocal(out=rr_[:, 1:2], in_=a1[:, 128:129]), r=[a1, rr_], w=[rr_])
                op("vector", lambda e, rr_=rr_: e.tensor_tensor(out=rr_[:, 2:3], in0=rr_[:, 1:2], in1=neglam, op=ALU.mult),
                   r=[rr_, lam], w=[rr_])
                op("vector", lambda e, a0=a0, rr_=rr_, t0_=t0_: e.tensor_scalar(out=t0_[:, :], in0=a0[:, 0:128], scalar1=rr_[:, 0:1],
                                                                             scalar2=None, op0=ALU.mult), r=[a0, rr_], w=[t0_])
                op("vector", lambda e, a1=a1, rr_=rr_, t0_=t0_, ob_=ob_: e.scalar_tensor_tensor(
                    out=ob_[:, :], in0=a1[:, 0:128], scalar=rr_[:, 2:3], in1=t0_[:, :], op0=ALU.mult, op1=ALU.add),
                   r=[a1, rr_, t0_], w=[ob_])
            for qt in range(2):
                rr_, t0_, ob_ = rr2[qt], t02[qt], ob2[qt]
                op("gpsimd", lambda e: e.memset(ast[:, 0:1], 0.0), w=[ast])
                op("scalar", lambda e, ob_=ob_: e.activation(out=ajunk[:, :], in_=ob_[:, :], func=AF.Square,
                                                             scale=1.0 / math.sqrt(128.0), accum_out=ast[:, 0:1]),
                   r=[ob_, ast], w=[ajunk, ast])
                op("scalar", lambda e: e.activation(out=ast[:, 0:1], in_=ast[:, 0:1], func=AF.Ln, bias=SUBLN_EPS, scale=1.0),
                   r=[ast], w=[ast])
                op("scalar", lambda e: e.activation(out=ast[:, 0:1], in_=ast[:, 0:1], func=AF.Exp, scale=-0.5),
                   r=[ast], w=[ast])
                op("vector", lambda e, ob_=ob_, t0_=t0_: e.scalar_tensor_tensor(out=t0_[:, :], in0=ob_[:, :], scalar=ast[:, 0:1],
                                                                               in1=bc_sub[:, :], op0=ALU.mult, op1=ALU.mult),
                   r=[ob_, ast, bc_sub], w=[t0_])
                ogq, sq_ = og[P][qt], sgb[P][qt]
                hs = slice(h * 128, (h + 1) * 128)
                op("gpsimd", lambda e, ogq=ogq, sq_=sq_, hs=hs, t0_=t0_: e.tensor_tensor(out=ogq[:, hs], in0=t0_[:, :], in1=sq_[:, hs],
                                                                                       op=ALU.mult),
                   r=[t0_, sq_], w=[ogq] if h == 0 else [], a=[ogq] if h else [])

        LA = 2
        for k_ in range(min(LA, len(steps))):
            emit_qk(*steps[k_])
        for i, (h, kt) in enumerate(steps):
            if i + LA < len(steps):
                emit_qk(*steps[i + LA])
            emit_rest(h, kt, i)
            if kt == n - 1:
                finalize(h)
                yield
            elif kt % 4 == 3:
                yield

    lasts = []

    def post_g(g):
        s3, P = g % 3, g % 2
        for j in range(2):
            it = 2 * g + j
            sl = slice(it * 128, (it + 1) * 128)
            yield from pm.run_g(og[P][j], h1t[s3][j], ptl[s3][j], hout, tmB)
            lasts.append(kb.dma("sync", out[sl, :], hout[:, :], r=[hout], key="stout"))
            yield

    def chain(*gs):
        for g_ in gs:
            yield from g_

    loads(0)
    issue_chunks(NB - 1)
    for _ in pre_g(0):
        pass
    for g in range(NG):
        side = []
        if g + 1 < NG:
            loads(g + 1)
        if g >= 1:
            side.append(post_g(g - 1))
        if g + 1 < NG:
            side.append(pre_g(g + 1))
        interleave([attn_g(g), chain(*side)])
    for _ in post_g(NG - 1):
        pass
    kb.mem.reset(m0)
    kb.rot = [0, 1, 2, 3, 4, 5, 6]
    return lasts[-1]


from concourse.bass_utils import run_bass_kernel_spmd

W_NAMES = ["norm_pre", "norm_post", "a_mu", "a_w_in", "a_w0", "a_w1", "a_w2", "a_a0", "a_a1", "a_a2", "a_k_k", "a_k_a",
           "a_r_k", "a_lnx_g", "a_lnx_b", "a_w_out", "kv_norm", "kv_w", "b_w_in", "b_lambda", "b_subln", "b_w_out",
           "ple_w", "ple_gate", "ple_norm"]
STRIP = {"a_mu", "a_w_in", "a_w0", "a_w1", "a_w2", "a_a0", "a_a1", "a_a2", "a_k_k", "a_k_a", "a_r_k", "a_lnx_g", "a_lnx_b",
         "a_w_out", "b_w_in", "b_lambda", "b_subln", "b_w_out"}


def prep_weights(inputs):
    w = {}
    for k in W_NAMES:
        v = np.ascontiguousarray(np.asarray(inputs[k], dtype=np.float32))
        if k in STRIP:
            v = v.reshape(v.shape[1:])
        if k == "a_r_k":
            v = v.reshape(-1)
        w[k] = v
    return w


def build_program(S, wshapes, cshapes):
    nc = bass.Bass("TRN2", target_bir_lowering=False)
    dram = {}
    dram["x"] = nc.dram_tensor("x", [S, D], F32, kind="ExternalInput").ap()
    dram["p"] = nc.dram_tensor("p", [2, S, PLE], F32, kind="ExternalInput").ap()
    for k, shp in wshapes.items():
        dram[k] = nc.dram_tensor(k, list(shp), F32, kind="ExternalInput").ap()
    for k, shp in cshapes.items():
        dram[k] = nc.dram_tensor(k, list(shp), F32, kind="ExternalInput").ap()
    out = nc.dram_tensor("out", [S, D], F32, kind="ExternalOutput").ap()
    yg = nc.dram_tensor("scr_yg", [S, D], BF16).ap()
    h1 = nc.dram_tensor("scr_h1", [S, D], F32).ap()
    kt = nc.dram_tensor("scr_kt", [8, 128, S], BF16).ap()
    vv = nc.dram_tensor("scr_v", [S, D], BF16).ap()
    kb = KB(nc, S)
    kb.load_consts(dram)
    phase_a1(kb, dram, yg)
    phase_a2(kb, dram, yg, h1, kt, vv)
    last = phase_b(kb, dram, h1, kt, vv, out)
    kb.c.emit(final_wait_ops=[last])
    return nc, kb


def make_const_inputs(S):
    c = {"c_" + k: v for k, v in make_consts().items()}
    cos, sin = rope_tables(S)
    c["c_cos"] = cos
    c["c_sin"] = sin
    return c


_CACHE = {}


def kernel(**inputs):
    x = np.asarray(inputs["x"], dtype=np.float32)
    p = np.asarray(inputs["p"], dtype=np.float32)
    B, S, _ = x.shape
    w = prep_weights(inputs)
    consts = make_const_inputs(S)
    key = (S,)
    if key not in _CACHE:
        _CACHE[key] = build_program(S, {k: v.shape for k, v in w.items()}, {k: v.shape for k, v in consts.items()})
    nc, kb = _CACHE[key]
    in_maps = []
    for b in range(B):
        m = {"x": np.ascontiguousarray(x[b]), "p": np.ascontiguousarray(p[:, b])}
        m.update(w)
        m.update(consts)
        in_maps.append(m)
    res = run_bass_kernel_spmd(nc, in_maps, core_ids=list(range(B)))
    return np.stack([np.asarray(r["out"], dtype=np.float32) for r in res.results], 0)
```
